# Optimizing a Trainium2 kernel written in Bass

```python
import math
import jax, jax.numpy as jnp
from jax import lax
import numpy as np

D_MODEL = 1024
BATCH = 8
SEQ = 4096
DEPTH = 1
DEC_BATCH = 32
DEC_SEQ = 16
PAST_LEN = 4096

CHUNK = 64
Q_BLOCK = 128
N_HEADS = 4
HEAD_DIM = 64
V_DIM = 2 * HEAD_DIM
ATTN_WIDTH = N_HEADS * V_DIM
CONV_WIDTH = D_MODEL - ATTN_WIDTH
MIX_WIDTH = ATTN_WIDTH + CONV_WIDTH
CONV_K = 31
QK_COLS = N_HEADS * 2 * HEAD_DIM
IN_COLS = 2 * QK_COLS + ATTN_WIDTH + 2 * CONV_WIDTH
D_FF = 4 * D_MODEL
N_MEM = 256
MEM_HEADS = 4
MEM_HEAD_DIM = 128
MEM_WIDTH = MEM_HEADS * MEM_HEAD_DIM
N_BUCKETS = 32
MAX_DISTANCE = 128
EPS = 1e-6
NEG_INF = -1e30

kernel_name = 'hybrid_diffattn_conformer_stream_step'


def rmsnorm(x, g):
    xf = x.astype(jnp.float32)
    y = xf * lax.rsqrt(jnp.mean(xf * xf, axis=-1, keepdims=True) + EPS)
    return (y * g.astype(jnp.float32)).astype(x.dtype)


def layernorm(x, g, b):
    xf = x.astype(jnp.float32)
    mu = jnp.mean(xf, axis=-1, keepdims=True)
    xc = xf - mu
    y = xc * lax.rsqrt(jnp.mean(xc * xc, axis=-1, keepdims=True) + EPS)
    return (y * g.astype(jnp.float32) + b.astype(jnp.float32)).astype(x.dtype)


def rel_bucket(rel):
    half = N_BUCKETS // 2
    max_exact = half // 2
    ret = jnp.where(rel > 0, half, 0)
    n = jnp.abs(rel)
    nf = jnp.maximum(n, 1).astype(jnp.float32)
    large = max_exact + (jnp.log(nf / max_exact) / math.log(MAX_DISTANCE / max_exact)
                         * (half - max_exact)).astype(jnp.int32)
    large = jnp.minimum(large, half - 1)
    return ret + jnp.where(n < max_exact, n, large)


def rel_bias(q_pos, k_pos, table):
    b = rel_bucket(k_pos[None, :] - q_pos[:, None])
    return jnp.transpose(table[b], (2, 0, 1)).astype(jnp.float32)


def chunk_mask(q_pos, k_pos):
    return (k_pos[None, :] // CHUNK) <= (q_pos[:, None] // CHUNK)


def diff_attend(q, k, v, bias, mask, lam):
    s = jnp.einsum('bqhmd,bkhmd->bhmqk', q, k).astype(jnp.float32) * (HEAD_DIM ** -0.5)
    s = jnp.where(mask, s + bias[None, :, None], NEG_INF)
    p = jax.nn.softmax(s, axis=-1)
    w = p[:, :, 0] - lam * p[:, :, 1]
    return jnp.einsum('bhqk,bkhe->bqhe', w.astype(v.dtype), v)


def depthwise_causal_conv(c_hist, w_conv, b_conv):
    out = lax.conv_general_dilated(
        c_hist, w_conv[:, None, :].astype(c_hist.dtype), window_strides=(1,), padding='VALID',
        dimension_numbers=('NWC', 'WIO', 'NWC'), feature_group_count=CONV_WIDTH)
    return out + b_conv


def memory_kv(mem, g_mem, w_mk, w_mv, g_mk):
    B, M, _ = mem.shape
    m = rmsnorm(mem, g_mem)
    mk = rmsnorm((m @ w_mk).reshape(B, M, MEM_HEADS, MEM_HEAD_DIM), g_mk)
    mv = (m @ w_mv).reshape(B, M, MEM_HEADS, MEM_HEAD_DIM)
    return mk, mv


def cross_attend(h, mk, mv, w_mq, g_mq, w_mo):
    B, T, _ = h.shape
    q = rmsnorm((h @ w_mq).reshape(B, T, MEM_HEADS, MEM_HEAD_DIM), g_mq)
    s = jnp.einsum('bqhd,bkhd->bhqk', q, mk.astype(q.dtype)).astype(jnp.float32) * (MEM_HEAD_DIM ** -0.5)
    p = jax.nn.softmax(s, axis=-1)
    o = jnp.einsum('bhqk,bkhd->bqhd', p.astype(h.dtype), mv.astype(h.dtype)).reshape(B, T, MEM_WIDTH)
    return o @ w_mo


def trunk_layer(x, k_past, v_past, c_past, mk, mv, rel_table, g_mix, w_in, g_q, g_k, lam_vec,
                g_sub, w_conv, b_conv, ln_g, ln_b, w_out, g_cross, w_mq, g_mq, w_mo,
                g_ffn, w_ff1, w_ff2, lam_init):
    B, T, _ = x.shape
    P = k_past.shape[1]
    h = rmsnorm(x, g_mix)
    z = h @ w_in
    zq, zk, zv, zu = jnp.split(z, [QK_COLS, 2 * QK_COLS, 2 * QK_COLS + ATTN_WIDTH], axis=-1)
    q = rmsnorm(zq.reshape(B, T, N_HEADS, 2, HEAD_DIM), g_q)
    k = rmsnorm(zk.reshape(B, T, N_HEADS, 2, HEAD_DIM), g_k)
    v = zv.reshape(B, T, N_HEADS, V_DIM)
    k_all = jnp.concatenate([k_past.astype(k.dtype), k], axis=1)
    v_all = jnp.concatenate([v_past.astype(v.dtype), v], axis=1)
    lp = lam_vec.astype(jnp.float32)
    lam = jnp.exp(jnp.sum(lp[0] * lp[1])) - jnp.exp(jnp.sum(lp[2] * lp[3])) + lam_init
    k_pos = jnp.arange(P + T, dtype=jnp.int32)
    if T > Q_BLOCK and T % Q_BLOCK == 0:
        nb = T // Q_BLOCK
        q_blocks = jnp.moveaxis(q.reshape(B, nb, Q_BLOCK, N_HEADS, 2, HEAD_DIM), 1, 0)

        def attend_block(args):
            i, q_blk = args
            q_pos = P + i * Q_BLOCK + jnp.arange(Q_BLOCK, dtype=jnp.int32)
            return diff_attend(q_blk, k_all, v_all, rel_bias(q_pos, k_pos, rel_table),
                               chunk_mask(q_pos, k_pos), lam)

        o = lax.map(attend_block, (jnp.arange(nb, dtype=jnp.int32), q_blocks))
        o = jnp.moveaxis(o, 0, 1).reshape(B, T, N_HEADS, V_DIM)
    else:
        q_pos = P + jnp.arange(T, dtype=jnp.int32)
        o = diff_attend(q, k_all, v_all, rel_bias(q_pos, k_pos, rel_table),
                        chunk_mask(q_pos, k_pos), lam)
    o = (rmsnorm(o, g_sub) * (1.0 - lam_init)).reshape(B, T, ATTN_WIDTH)
    a, gate = jnp.split(zu, 2, axis=-1)
    c = a * jax.nn.sigmoid(gate)
    c_hist = jnp.concatenate([c_past.astype(c.dtype), c], axis=1)
    cv = jax.nn.silu(layernorm(depthwise_causal_conv(c_hist, w_conv, b_conv), ln_g, ln_b))
    x = x + jnp.concatenate([o, cv], axis=-1) @ w_out
    x = x + cross_attend(rmsnorm(x, g_cross), mk, mv, w_mq, g_mq, w_mo)
    hf = rmsnorm(x, g_ffn)
    x = x + jnp.square(jax.nn.relu(hf @ w_ff1)) @ w_ff2
    return x, k, v, c_hist[:, -(CONV_K - 1):]


def setup_inputs(seed: int = 0) -> dict:
    key = jax.random.key(seed)
    ks = jax.random.split(key, 40)

    def nrm(k, shape, scale):
        return jax.random.normal(k, shape, jnp.float32) * scale

    def gain(k, shape):
        return 1.0 + 0.05 * jax.random.normal(k, shape, jnp.float32)

    return {
        'x_prompt': nrm(ks[0], (BATCH, SEQ, D_MODEL), 1.0),
        'x_sample': nrm(ks[1], (DEC_BATCH, DEC_SEQ, D_MODEL), 1.0),
        'cache_k': nrm(ks[2], (DEPTH, DEC_BATCH, PAST_LEN, N_HEADS, 2, HEAD_DIM), 1.0),
        'cache_v': nrm(ks[3], (DEPTH, DEC_BATCH, PAST_LEN, N_HEADS, V_DIM), 1.0),
        'cache_conv': nrm(ks[4], (DEPTH, DEC_BATCH, CONV_K - 1, CONV_WIDTH), 0.5),
        'cache_mem_k': nrm(ks[5], (DEPTH, DEC_BATCH, N_MEM, MEM_HEADS, MEM_HEAD_DIM), 1.0),
        'cache_mem_v': nrm(ks[6], (DEPTH, DEC_BATCH, N_MEM, MEM_HEADS, MEM_HEAD_DIM), 1.0),
        'mem_prompt': nrm(ks[7], (BATCH, N_MEM, D_MODEL), 1.0),
        'rel_table': nrm(ks[8], (N_BUCKETS, N_HEADS), 0.5),
        'g_mix': gain(ks[9], (DEPTH, D_MODEL)),
        'w_in': nrm(ks[10], (DEPTH, D_MODEL, IN_COLS), D_MODEL ** -0.5),
        'g_q': gain(ks[11], (DEPTH, HEAD_DIM)),
        'g_k': gain(ks[12], (DEPTH, HEAD_DIM)),
        'lam_vec': nrm(ks[13], (DEPTH, 4, HEAD_DIM), 0.1),
        'g_sub': gain(ks[14], (DEPTH, V_DIM)),
        'w_conv': nrm(ks[15], (DEPTH, CONV_K, CONV_WIDTH), CONV_K ** -0.5),
        'b_conv': nrm(ks[16], (DEPTH, CONV_WIDTH), 0.02),
        'ln_g': gain(ks[17], (DEPTH, CONV_WIDTH)),
        'ln_b': nrm(ks[18], (DEPTH, CONV_WIDTH), 0.02),
        'w_out': nrm(ks[19], (DEPTH, MIX_WIDTH, D_MODEL), MIX_WIDTH ** -0.5),
        'g_cross': gain(ks[20], (DEPTH, D_MODEL)),
        'g_mem': gain(ks[21], (DEPTH, D_MODEL)),
        'w_mq': nrm(ks[22], (DEPTH, D_MODEL, MEM_WIDTH), D_MODEL ** -0.5),
        'w_mk': nrm(ks[23], (DEPTH, D_MODEL, MEM_WIDTH), D_MODEL ** -0.5),
        'w_mv': nrm(ks[24], (DEPTH, D_MODEL, MEM_WIDTH), D_MODEL ** -0.5),
        'g_mq': gain(ks[25], (DEPTH, MEM_HEAD_DIM)),
        'g_mk': gain(ks[26], (DEPTH, MEM_HEAD_DIM)),
        'w_mo': nrm(ks[27], (DEPTH, MEM_WIDTH, D_MODEL), MEM_WIDTH ** -0.5),
        'g_ffn': gain(ks[28], (DEPTH, D_MODEL)),
        'w_ff1': nrm(ks[29], (DEPTH, D_MODEL, D_FF), D_MODEL ** -0.5),
        'w_ff2': nrm(ks[30], (DEPTH, D_FF, D_MODEL), D_FF ** -0.5),
    }


def reference(x_prompt, x_sample, cache_k, cache_v, cache_conv, cache_mem_k, cache_mem_v,
              mem_prompt, rel_table, g_mix, w_in, g_q, g_k, lam_vec, g_sub, w_conv, b_conv,
              ln_g, ln_b, w_out, g_cross, g_mem, w_mq, w_mk, w_mv, g_mq, g_mk, w_mo,
              g_ffn, w_ff1, w_ff2):
    B = x_prompt.shape[0]
    yp, ys = x_prompt, x_sample
    empty_k = jnp.zeros((B, 0, N_HEADS, 2, HEAD_DIM), x_prompt.dtype)
    empty_v = jnp.zeros((B, 0, N_HEADS, V_DIM), x_prompt.dtype)
    zero_conv = jnp.zeros((B, CONV_K - 1, CONV_WIDTH), x_prompt.dtype)
    kp_l, vp_l, cp_l, mkp_l, mvp_l, ks_l, vs_l, cs_l = [], [], [], [], [], [], [], []
    for l in range(DEPTH):
        lam_init = 0.8 - 0.6 * math.exp(-0.3 * l)
        lw = (rel_table, g_mix[l], w_in[l], g_q[l], g_k[l], lam_vec[l], g_sub[l], w_conv[l],
              b_conv[l], ln_g[l], ln_b[l], w_out[l], g_cross[l], w_mq[l], g_mq[l], w_mo[l],
              g_ffn[l], w_ff1[l], w_ff2[l])
        mk_p, mv_p = memory_kv(mem_prompt, g_mem[l], w_mk[l], w_mv[l], g_mk[l])
        yp, kp, vp, cp = trunk_layer(yp, empty_k, empty_v, zero_conv, mk_p, mv_p, *lw, lam_init)
        ys, kn, vn, cn = trunk_layer(ys, cache_k[l], cache_v[l], cache_conv[l],
                                     cache_mem_k[l], cache_mem_v[l], *lw, lam_init)
        kp_l.append(kp); vp_l.append(vp); cp_l.append(cp)
        mkp_l.append(mk_p); mvp_l.append(mv_p)
        ks_l.append(kn); vs_l.append(vn); cs_l.append(cn)
    return (yp, ys, jnp.stack(kp_l), jnp.stack(vp_l), jnp.stack(cp_l), jnp.stack(mkp_l),
            jnp.stack(mvp_l), jnp.stack(ks_l), jnp.stack(vs_l), jnp.stack(cs_l))
```

```python
import math
import types
import numpy as np
import concourse.bass as bass
import concourse.mybir as mybir
from concourse.bass_utils import run_bass_kernel_spmd

F32 = mybir.dt.float32
BF16 = mybir.dt.bfloat16
AF = mybir.ActivationFunctionType
ALU = mybir.AluOpType
AX = mybir.AxisListType

D = 1024
SEQ = 4096
NT = 32
PAST = 4096
EPS = 1e-6
LAM_INIT = 0.8 - 0.6 * math.exp(-0.3 * 0)
NCORES = 8
COMPUTE = ("pe", "act", "dve", "pool")
NDMA_SEMS = 12
_DBG_KIND = {}


class Res:
    __slots__ = ("name", "w", "r", "excl")

    def __init__(self, name, excl=False):
        self.name = name
        self.w = None
        self.r = []
        self.excl = excl


class Op:
    __slots__ = ("eng", "fn", "deps", "is_dma", "dsem", "dval", "inc_val", "needs_inc", "waits")

    def __init__(self, eng, fn, is_dma):
        self.eng = eng
        self.fn = fn
        self.deps = []
        self.is_dma = is_dma
        self.dsem = None
        self.dval = None
        self.inc_val = None
        self.needs_inc = False
        self.waits = []


class Sched:
    def __init__(self, nc):
        self.nc = nc
        self.ops = {e: [] for e in ("pe", "act", "dve", "pool", "sp")}
        self.dma_rr = {"sp": 0, "pool": 0}
        self.dma_last = {}
        self.pending_barrier = {}

    def R(self, name, excl=False):
        return Res(name, excl)

    def barrier(self):
        deps = []
        for eng in COMPUTE:
            for op in reversed(self.ops[eng]):
                if not op.is_dma:
                    deps.append(op)
                    break
        deps.extend(self.dma_last.values())
        for eng in self.ops:
            self.pending_barrier[eng] = list(deps)

    @staticmethod
    def _freeze(fn):
        if fn.__closure__ is None:
            return fn
        cells = []
        for c in fn.__closure__:
            try:
                cells.append(types.CellType(c.cell_contents))
            except ValueError:
                cells.append(c)
        return types.FunctionType(fn.__code__, fn.__globals__, fn.__name__, fn.__defaults__, tuple(cells))

    def _add(self, eng, fn, reads, writes, is_dma=False):
        fn = self._freeze(fn)
        op = Op(eng, fn, is_dma)
        deps = []
        ex = [r for r in reads if r.excl]
        if ex:
            writes = list(writes) + [r for r in ex if r not in writes]
            reads = [r for r in reads if not r.excl]
        for r in reads:
            if r.w is not None:
                deps.append(r.w)
        for w in writes:
            if w.w is not None:
                deps.append(w.w)
            deps.extend(w.r)
        pb = self.pending_barrier.pop(eng, None)
        if pb:
            deps.extend(pb)
        if is_dma:
            slot = self.dma_rr[eng]
            self.dma_rr[eng] = (slot + 1) % NDMA_SEMS
            prev = self.dma_last.get((eng, slot))
            if prev is not None:
                deps.append(prev)
                op.dval = prev.dval + 16
            else:
                op.dval = 16
            op.dsem = (eng, slot)
            self.dma_last[(eng, slot)] = op
        seen = set()
        for d in deps:
            if d is op or id(d) in seen:
                continue
            seen.add(id(d))
            op.deps.append(d)
        for r in reads:
            r.r.append(op)
        for w in writes:
            w.w = op
            w.r = []
        self.ops[eng].append(op)
        return op

    def pe(self, fn, reads=(), writes=()):
        return self._add("pe", fn, reads, writes)

    def act(self, fn, reads=(), writes=()):
        return self._add("act", fn, reads, writes)

    def dve(self, fn, reads=(), writes=()):
        return self._add("dve", fn, reads, writes)

    def pool(self, fn, reads=(), writes=()):
        return self._add("pool", fn, reads, writes)

    def dma(self, fn, reads=(), writes=(), q="sp"):
        return self._add(q, fn, reads, writes, is_dma=True)

    def finalize(self):
        nc = self.nc
        for eng, lst in self.ops.items():
            for op in lst:
                for d in op.deps:
                    if d.is_dma:
                        continue
                    if d.eng == eng and eng == "pe":
                        continue
                    d.needs_inc = True
        for eng in COMPUTE:
            c = 0
            for op in self.ops[eng]:
                if op.is_dma:
                    continue
                if op.needs_inc:
                    c += 1
                    op.inc_val = c
        for eng, lst in self.ops.items():
            waited = {}
            for op in lst:
                need = {}
                for d in op.deps:
                    if d.is_dma:
                        key = ("dma",) + d.dsem
                        val = d.dval
                    else:
                        if d.eng == eng and eng == "pe":
                            continue
                        key = ("eng", d.eng)
                        val = d.inc_val
                    if need.get(key, 0) < val:
                        need[key] = val
                for key, val in need.items():
                    if waited.get(key, 0) >= val:
                        continue
                    waited[key] = val
                    op.waits.append((key, val))
        sems = {}
        for eng in COMPUTE:
            sems[("eng", eng)] = nc.alloc_semaphore(name=f"sem_{eng}")
        for q in ("sp", "pool"):
            for s in range(NDMA_SEMS):
                sems[("dma", q, s)] = nc.alloc_semaphore(name=f"sem_dma_{q}_{s}")
        final = [(("dma", q, slot), op.dval) for (q, slot), op in self.dma_last.items()]
        ops = self.ops

        def replay(eng_name, e):
            for op in ops[eng_name]:
                for key, val in op.waits:
                    e.wait_ge(sems[key], val)
                ins = op.fn(e)
                if op.is_dma:
                    ins.then_inc(sems[("dma",) + op.dsem], 16)
                elif op.needs_inc:
                    ins.then_inc(sems[("eng", op.eng)], 1)

        with nc.Block() as block:
            @block.tensor
            def _(e):
                replay("pe", e)

            @block.scalar
            def _(e):
                replay("act", e)

            @block.vector
            def _(e):
                replay("dve", e)

            @block.gpsimd
            def _(e):
                replay("pool", e)

            @block.sync
            def _(e):
                replay("sp", e)
                for key, val in final:
                    e.wait_ge(sems[key], val)


class Buf:
    def __init__(self, S, nc, name, shape, dtype, stack=None):
        if stack is not None:
            self.t = stack.enter_context(nc.sbuf_tensor(name, shape, dtype))
        else:
            self.t = nc.alloc_sbuf_tensor(name, shape, dtype)
        self.r = S.R(name)


class Rot:
    def __init__(self, items):
        self.items = items
        self.i = 0

    def next(self):
        x = self.items[self.i % len(self.items)]
        self.i += 1
        return x


def rel_bucket_np(rel):
    half, max_exact = 16, 8
    n = np.abs(rel)
    nf = np.maximum(n, 1).astype(np.float32)
    large = (max_exact + (np.log(nf / np.float32(max_exact)) / np.float32(math.log(128 / max_exact))
                          * np.float32(half - max_exact)).astype(np.int32))
    large = np.minimum(large, half - 1)
    return np.where(rel > 0, half, 0) + np.where(n < max_exact, n, large)


def build_program():
    from contextlib import ExitStack
    nc = bass.Bass("TRN2", target_bir_lowering=False)
    S = Sched(nc)

    def din(name, shape):
        return nc.dram_tensor(name, shape, F32, kind="ExternalInput").ap()

    def dout(name, shape):
        return nc.dram_tensor(name, shape, F32, kind="ExternalOutput").ap()

    xp = din("xp", [SEQ, D]); xs = din("xs", [64, D])
    ck = din("ck", [4, PAST, 512]); cv = din("cv", [4, PAST, 512]); cc = din("cc", [4, 30, 512])
    cmk = din("cmk", [4, 256, 512]); cmv = din("cmv", [4, 256, 512]); memp = din("memp", [256, D])
    rel_table = din("rel_table", [32, 4]); ohE = din("ohE", [32, 383])
    g_mix = din("g_mix", [1, D]); w_in = din("w_in", [D, 2560]); g_q = din("g_q", [1, 64]); g_k = din("g_k", [1, 64])
    lam_vec = din("lam_vec", [1, 256]); g_sub = din("g_sub", [1, 128]); w_conv = din("w_conv", [31, 512])
    b_conv = din("b_conv", [1, 512]); ln_g = din("ln_g", [1, 512]); ln_b = din("ln_b", [1, 512])
    w_out = din("w_out", [D, D]); g_cross = din("g_cross", [1, D]); g_mem = din("g_mem", [1, D])
    w_mq = din("w_mq", [D, 512]); w_mk = din("w_mk", [D, 512]); w_mv = din("w_mv", [D, 512])
    g_mq = din("g_mq", [1, 128]); g_mk = din("g_mk", [1, 128]); w_mo = din("w_mo", [512, D])
    g_ffn = din("g_ffn", [1, D]); w_ff1 = din("w_ff1", [D, 4096]); w_ff2 = din("w_ff2", [4096, D])
    yp = dout("yp", [SEQ, D]); ys = dout("ys", [64, D]); kp = dout("kp", [SEQ, 512]); vp = dout("vp", [SEQ, 512])
    cpo = dout("cpo", [30, 512]); mkp = dout("mkp", [256, 512]); mvp = dout("mvp", [256, 512])
    kso = dout("kso", [64, 512]); vso = dout("vso", [64, 512]); cso = dout("cso", [4, 30, 512])
    X1 = nc.dram_tensor("X1", [SEQ + 64, D], F32, **_DBG_KIND).ap()
    X2 = nc.dram_tensor("X2", [SEQ + 64, D], F32, **_DBG_KIND).ap()
    r_x1 = [S.R(f"x1_{t}") for t in range(NT + 1)]
    WB = {}
    for nm_, src_, shp_ in (("wmk", w_mk, [D, 512]), ("wmv", w_mv, [D, 512]), ("wmq", w_mq, [D, 512]), ("wmo", w_mo, [512, D]),
                            ("wff1", w_ff1, [D, 4096]), ("wff2", w_ff2, [4096, D])):
        WB[nm_] = (nc.dram_tensor(nm_ + "_bf", shp_, BF16).ap(), src_, S.R(nm_ + "_bf"), shp_)

    def precast_gen():
        for nm_, nchunk in (("wmk", 1), ("wmv", 1), ("wmq", 1), ("wmo", 1), ("wff1", 4), ("wff2", 4)):
            dst_, src_, res_, shp_ = WB[nm_]
            rows = shp_[0] // nchunk
            for i in range(nchunk):
                S.dma(lambda e, dst_=dst_, src_=src_, i=i, rows=rows: e.dma_start(out=dst_[i * rows:(i + 1) * rows, :], in_=src_[i * rows:(i + 1) * rows, :]),
                      writes=[res_], q="pool")
                yield

    def load_wb(dst, nm_, kchunks, piece=None):
        wsrc, _, res_, shp_ = WB[nm_]
        ncols = shp_[1]
        piece = piece or ncols
        wv = wsrc.rearrange("(k p) c -> p k c", p=128)
        for cs in range(0, ncols, piece):
            S.dma(lambda e, cs=cs: e.dma_start(out=dst.t[:, :, cs:cs + piece], in_=wv[:, :, cs:cs + piece]), reads=[res_], writes=[dst.r])
    r_x2 = [S.R(f"x2_{t}") for t in range(NT + 1)]

    banks = [nc.alloc_psum_tensor(f"bank{i}", [128, 512], F32) for i in range(8)]
    rb = [S.R(f"bank{i}", excl=True) for i in range(8)]

    def B(name, shape, dtype, stack=None):
        return Buf(S, nc, name, shape, dtype, stack)

    identf = B("identf", [128, 128], F32)
    identb = B("identb", [128, 128], BF16)
    onesb = B("onesb", [128, 128], BF16)
    mhalf = B("mhalf", [128, 8], F32)
    gcols = B("gcols", [128, 4, 8], F32)
    gq8 = B("gq8", [128, 64], F32)
    gkb = B("gkb", [128, 64], F32)
    gsubb = B("gsubb", [128, 128], F32)
    gmqb = B("gmqb", [128, 128], F32)
    gmkb = B("gmkb", [128, 128], F32)
    lamv = B("lamv", [128, 256], F32)
    lamt = B("lamt", [128, 8], F32)
    wconv = B("wconv", [128, 4, 31], F32)
    cvec = B("cvec", [128, 3, 4], F32)
    biasT = B("biasT", [128, 2, 4, 128], F32)
    biasHL = B("biasHL", [128, 2, 2, 4, 128], BF16)
    Esb = B("Esb", [32, 383], F32)
    tabsb = B("tabsb", [32, 4], F32)
    cfar = B("cfar", [128, 4], F32)
    stg = B("stg", [32, 128], F32)

    def setup():
        S.pool(lambda e: e.memset(identf.t[:], 0.0), writes=[identf.r])
        S.pool(lambda e: e.affine_select(out=identf.t[:], in_=identf.t[:], pattern=[[-1, 128]], compare_op=ALU.not_equal,
                                         fill=1.0, base=0, channel_multiplier=1), reads=[identf.r], writes=[identf.r])
        S.pool(lambda e: e.tensor_copy(out=identb.t[:], in_=identf.t[:]), reads=[identf.r], writes=[identb.r])
        S.pool(lambda e: e.memset(onesb.t[:], 1.0), writes=[onesb.r])
        S.pool(lambda e: e.memset(mhalf.t[:], -0.5), writes=[mhalf.r])
        def loadT(dst_ap, src_ap, n, dst_res):
            S.dma(lambda e: e.dma_start(out=stg.t[0:n, :], in_=src_ap), writes=[stg.r])
            S.pe(lambda e: e.transpose(out=banks[7][:, 0:n], in_=stg.t[0:n, :], identity=identf.t[0:n, 0:n]), reads=[stg.r, identf.r], writes=[rb[7]])
            S.act(lambda e: e.activation(out=dst_ap, in_=banks[7][:, 0:n], func=AF.Copy), reads=[rb[7]], writes=[dst_res])
        for i, g in enumerate((g_mix, g_cross, g_ffn, g_mem)):
            loadT(gcols.t[:, i, :], g[0, :].rearrange("(k p) -> k p", p=128), 8, gcols.r)
        S.dma(lambda e: e.dma_start(out=gq8.t[:], in_=g_q[0:1, :].partition_broadcast(128)), writes=[gq8.r])
        S.dma(lambda e: e.dma_start(out=gkb.t[:], in_=g_k[0:1, :].partition_broadcast(128)), writes=[gkb.r])
        S.dma(lambda e: e.dma_start(out=gsubb.t[:], in_=g_sub[0:1, :].partition_broadcast(128)), writes=[gsubb.r])
        S.dma(lambda e: e.dma_start(out=gmqb.t[:], in_=g_mq[0:1, :].partition_broadcast(128)), writes=[gmqb.r])
        S.dma(lambda e: e.dma_start(out=gmkb.t[:], in_=g_mk[0:1, :].partition_broadcast(128)), writes=[gmkb.r])
        S.dma(lambda e: e.dma_start(out=lamv.t[:], in_=lam_vec[0:1, :].partition_broadcast(128)), writes=[lamv.r])
        for cb in range(4):
            loadT(wconv.t[:, cb, :], w_conv[:, cb * 128:(cb + 1) * 128], 31, wconv.r)
        for i, v in enumerate((b_conv, ln_g, ln_b)):
            loadT(cvec.t[:, i, :], v[0, :].rearrange("(c p) -> c p", p=128), 4, cvec.r)
        S.dma(lambda e: e.dma_start(out=Esb.t[:], in_=ohE[:, :]), writes=[Esb.r])
        S.dma(lambda e: e.dma_start(out=tabsb.t[:], in_=rel_table[:, :]), writes=[tabsb.r])
        S.dma(lambda e: e.dma_start(out=cfar.t[:], in_=rel_table[15:16, :].partition_broadcast(128)), writes=[cfar.r])
        S.dve(lambda e: e.tensor_scalar(out=gq8.t[:], in0=gq8.t[:], scalar1=0.125, scalar2=None, op0=ALU.mult),
              reads=[gq8.r], writes=[gq8.r])
        S.dve(lambda e: e.tensor_scalar(out=gsubb.t[:], in0=gsubb.t[:], scalar1=1.0 - LAM_INIT, scalar2=None, op0=ALU.mult),
              reads=[gsubb.r], writes=[gsubb.r])
        S.dve(lambda e: e.tensor_scalar(out=gmqb.t[:], in0=gmqb.t[:], scalar1=128.0 ** -0.5, scalar2=None, op0=ALU.mult),
              reads=[gmqb.r], writes=[gmqb.r])
        S.dve(lambda e: e.tensor_scalar(out=cvec.t[:, 1:3, :], in0=cvec.t[:, 1:3, :], scalar1=0.5, scalar2=None, op0=ALU.mult),
              reads=[cvec.r], writes=[cvec.r])
        lt = lamt.t
        S.dve(lambda e: e.tensor_tensor(out=lamv.t[:, 0:64], in0=lamv.t[:, 0:64], in1=lamv.t[:, 64:128], op=ALU.mult),
              reads=[lamv.r], writes=[lamv.r])
        S.dve(lambda e: e.tensor_tensor(out=lamv.t[:, 128:192], in0=lamv.t[:, 128:192], in1=lamv.t[:, 192:256], op=ALU.mult),
              reads=[lamv.r], writes=[lamv.r])
        S.dve(lambda e: e.reduce_sum(out=lt[:, 0:1], in_=lamv.t[:, 0:64], axis=AX.X), reads=[lamv.r], writes=[lamt.r])
        S.dve(lambda e: e.reduce_sum(out=lt[:, 1:2], in_=lamv.t[:, 128:192], axis=AX.X), reads=[lamv.r], writes=[lamt.r])
        S.act(lambda e: e.activation(out=lt[:, 2:4], in_=lt[:, 0:2], func=AF.Exp), reads=[lamt.r], writes=[lamt.r])
        S.dve(lambda e: e.tensor_tensor(out=lt[:, 4:5], in0=lt[:, 2:3], in1=lt[:, 3:4], op=ALU.subtract),
              reads=[lamt.r], writes=[lamt.r])
        S.dve(lambda e: e.tensor_scalar(out=lt[:, 5:6], in0=lt[:, 4:5], scalar1=LAM_INIT, scalar2=-1.0, op0=ALU.add, op1=ALU.mult),
              reads=[lamt.r], writes=[lamt.r])
        for delta in range(2):
            off = 255 - 128 * delta
            bk = banks[delta]
            for q in range(128):
                S.pe(lambda e, q=q, off=off, bk=bk: e.matmul(bk[:, q * 4:(q + 1) * 4], lhsT=Esb.t[:, off - q:off - q + 128],
                                                            rhs=tabsb.t[:, :], start=True, stop=True, skip_group_check=True),
                     reads=[Esb.r, tabsb.r], writes=[rb[delta]])
            S.dve(lambda e, delta=delta, bk=bk: e.tensor_tensor(
                out=biasT.t[:, delta, :, :].rearrange("p h q -> p q h"),
                in0=bk[:, :].rearrange("p (q h) -> p q h", h=4),
                in1=cfar.t[:, :].unsqueeze(1).to_broadcast([128, 128, 4]), op=ALU.subtract),
                reads=[rb[delta], cfar.r], writes=[biasT.r])
        S.dve(lambda e: e.memset(biasT.t[64:128, 0, :, 0:64], -30000.0), reads=[biasT.r], writes=[biasT.r])
        S.dve(lambda e: e.tensor_copy(out=biasHL.t[:, 0], in_=biasT.t[:]), reads=[biasT.r], writes=[biasHL.r])
        S.dve(lambda e: e.tensor_tensor(out=biasHL.t[:, 1], in0=biasT.t[:], in1=biasHL.t[:, 0], op=ALU.subtract),
              reads=[biasT.r, biasHL.r], writes=[biasHL.r])

    def load_w_bf16(dst, dram_ap, kchunks, cols, c0=0, ncols=None, piece=None):
        ncols = cols if ncols is None else ncols
        piece = piece or ncols
        wv = dram_ap.rearrange("(k p) c -> p k c", p=128)
        for cs in range(0, ncols, piece):
            S.dma(lambda e, cs=cs: e.dma_start(out=dst.t[:, :, cs:cs + piece], in_=wv[:, :, c0 + cs:c0 + cs + piece]),
                  writes=[dst.r], q="pool")

    class TT:
        def __init__(self, idx, rows, r0, coff):
            self.idx, self.rows, self.r0, self.coff = idx, rows, r0, coff

    def norm_gen(tiles, src_fn, src_res_fn, gi, hT, xin_rot, hn_rot, ss_rot, junk, pbank):
        for tl in tiles:
            rows = tl.rows
            xb = xin_rot.next(); hn = hn_rot.next(); ss = ss_rot.next()
            S.dma(lambda e, xb=xb, tl=tl: e.dma_start(out=xb.t[0:tl.rows, :], in_=src_fn(tl)), reads=src_res_fn(tl), writes=[xb.r])
            S.act(lambda e, xb=xb, ss=ss, rows=rows: e.activation(out=junk.t[0:rows, :], in_=xb.t[0:rows, :], func=AF.Square,
                                                                  accum_out=ss.t[0:rows, 0:1]),
                  reads=[xb.r], writes=[junk.r, ss.r])
            S.dve(lambda e, ss=ss, rows=rows: e.tensor_scalar(out=ss.t[0:rows, 0:1], in0=ss.t[0:rows, 0:1], scalar1=1.0 / D, scalar2=EPS,
                                                              op0=ALU.mult, op1=ALU.add), reads=[ss.r], writes=[ss.r])
            S.pool(lambda e, ss=ss, rows=rows: e.tensor_tensor(out=ss.t[0:rows, 1:2], in0=ss.t[0:rows, 0:1], in1=mhalf.t[0:rows, 0:1], op=ALU.pow),
                   reads=[ss.r, mhalf.r], writes=[ss.r])
            S.dve(lambda e, xb=xb, hn=hn, ss=ss, rows=rows: e.tensor_scalar(out=hn.t[0:rows, :], in0=xb.t[0:rows, :], scalar1=ss.t[0:rows, 1:2],
                                                                            scalar2=None, op0=ALU.mult), reads=[xb.r, ss.r], writes=[hn.r])
            yield
            pb = banks[pbank][:].bitcast(BF16)
            for kc in range(8):
                S.pe(lambda e, kc=kc, hn=hn, rows=rows: e.transpose(out=pb[:, kc * 128:kc * 128 + rows], in_=hn.t[0:rows, kc * 128:(kc + 1) * 128],
                                                                   identity=identb.t[0:rows, 0:rows]),
                     reads=[hn.r, identb.r], writes=[rb[pbank]])
            S.dve(lambda e, tl=tl, rows=rows, pb=pb: e.tensor_tensor(out=hT.t[:, :, tl.coff:tl.coff + rows],
                                                                     in0=pb[:, :].rearrange("p (k c) -> p k c", c=128)[:, :, 0:rows],
                                                                     in1=gcols.t[:, gi, :].unsqueeze(2).to_broadcast([128, 8, rows]), op=ALU.mult),
                  reads=[rb[pbank], gcols.r], writes=[hT.r])
            yield

    def run(gen):
        for _ in gen:
            pass

    def step(gen, n=1):
        for _ in range(n):
            try:
                next(gen)
            except StopIteration:
                return False
        return True

    def stage_norm(*a):
        run(norm_gen(*a))

    def rstd_groups(ps_ap, ps_res, rows, ngroups, gsize, sq, ssb):
        S.act(lambda e: e.activation(out=sq.t[0:rows, :], in_=ps_ap, func=AF.Square), reads=[ps_res], writes=[sq.r])
        S.dve(lambda e: e.reduce_sum(out=ssb.t[0:rows, 0:ngroups], in_=sq.t[0:rows, :].rearrange("p (g d) -> p g d", d=gsize), axis=AX.X),
              reads=[sq.r], writes=[ssb.r])
        S.dve(lambda e: e.tensor_scalar(out=ssb.t[0:rows, 0:ngroups], in0=ssb.t[0:rows, 0:ngroups], scalar1=1.0 / gsize, scalar2=EPS,
                                        op0=ALU.mult, op1=ALU.add), reads=[ssb.r], writes=[ssb.r])
        S.pool(lambda e: e.tensor_tensor(out=ssb.t[0:rows, ngroups:2 * ngroups], in0=ssb.t[0:rows, 0:ngroups], in1=mhalf.t[0:rows, 0:ngroups],
                                         op=ALU.pow), reads=[ssb.r, mhalf.r], writes=[ssb.r])

    setup()
    with ExitStack() as stA:
        def BA(name, shape, dtype):
            return B(name, shape, dtype, stA)
        GA = 256
        KT = BA("KT", [128, NT, 4, 128], BF16)
        VA = BA("VA", [128, NT, 4, 132], BF16)
        r_kt = [S.R(f"kt{t}") for t in range(NT)]
        r_va = [S.R(f"va{t}") for t in range(NT)]
        win = BA("win", [128, 8, 2560], BF16)
        wout = BA("wout", [128, 8, 1024], BF16)
        xin_rot = Rot([BA(f"xinA{i}", [128, D], F32) for i in range(2)])
        hn_rot = Rot([BA("hnA0", [128, D], BF16)])
        ss_rot = Rot([BA(f"ssA{i}", [128, 2], F32) for i in range(2)])
        junk = hn_rot.items[0]
        hT = BA("hTA", [128, 8, GA], BF16)
        sq = BA("sqA", [128, 512], F32)
        ssq = BA("ssq", [128, 16], F32); ssk = BA("ssk", [128, 16], F32)
        t1 = sq
        knf_rot = Rot([BA(f"knf{i}", [128, 512], F32) for i in range(1)])
        vf_rot = Rot([BA(f"vf{i}", [128, 512], F32) for i in range(1)])
        qnb = BA("qnb", [128, 512], BF16); knb = BA("knb", [128, 512], BF16)
        QT = BA("QT", [128, 4, 64], BF16)
        QB = [BA(f"QB{i}", [128, 4, 2, GA], BF16) for i in range(2)]
        chist2 = BA("chist2", [128, 4, 30 + GA], F32)
        chist = BA("chist", [128, 4, 30 + GA], F32)
        th_rot = Rot([BA(f"thA{i}", [128, GA], F32) for i in range(2)])
        thg_rot = Rot([BA("thG0", [128, GA], F32)])
        acc = BA("accA", [128, 4, GA], F32)
        acc_r = [S.R(f"acc{cb}") for cb in range(4)]
        accb = BA("accb", [128, 4, GA], BF16); sqb = BA("sqb", [128, 4, GA], BF16)
        mean = BA("meanA", [128, GA], F32); rstd = BA("rstdA", [128, GA], F32)
        y_rot = Rot([BA(f"yA{i}", [128, GA], F32) for i in range(2)])
        mixT = BA("mixT", [128, 8, GA], BF16)
        mixT_r = [S.R(f"mixT{c}") for c in range(8)]
        PT_rot = Rot([BA(f"PT{i}", [128, 2, GA], BF16) for i in range(2)])
        ep_rl = Rot([BA(f"eprl{i}", [128, 8], F32) for i in range(2)])
        o0_rot = Rot([BA(f"o0_{i}", [128, 128], F32) for i in range(2)])
        o_rot = Rot([BA(f"o_{i}", [128, 128], F32) for i in range(2)])
        otok_rot = Rot([BA(f"otok{i}", [128, 4, 128], BF16) for i in range(2)])
        ojunk = BA("ojunk", [128, 128], BF16)
        ctmp = BA("ctmp", [30, 512], F32)
        QBD = BA("QBD", [128, 4, 4, 32], BF16)
        KnT = BA("KnT", [128, 4, 64], BF16)
        class View:
            def __init__(self, ap, name):
                self.t = ap
                self.r = S.R(name)
        vaflat = VA.t[:].rearrange("p t h e -> p (t h e)")
        _o = [0]

        def carve(n, pat, name, **kw):
            ap = vaflat[:, _o[0]:_o[0] + n].rearrange(pat, **kw)
            _o[0] += n
            return View(ap, name)
        ktf32 = KT.t[:].rearrange("p t h k -> p (t h k)").bitcast(F32)
        kraw_rot = Rot([View(ktf32[:, i * 2048:(i + 1) * 2048].rearrange("p (t c) -> p t c", c=512), f"kraw{i}") for i in range(2)])
        vraw_rot = Rot([View(ktf32[:, 4096 + i * 2048:4096 + (i + 1) * 2048].rearrange("p (t c) -> p t c", c=512), f"vraw{i}") for i in range(2)])
        vs_rot = Rot([carve(2112, "p (t h e) -> p t h e", f"vs{i}", h=4, e=132) for i in range(2)])
        ktc_rot = Rot([carve(2048, "p (t h k) -> p t h k", f"ktc{i}", h=4, k=128) for i in range(2)])
        pts_rot = Rot([carve(512, "p (t c) -> p t c", f"pts{i}", c=128) for i in range(2)])
        Vn = View(vaflat[0:16, _o[0]:_o[0] + 2112].rearrange("p (s h e) -> p s h e", s=4, h=4), "Vn")
        nears = BA("nears", [128, 128], F32)

        load_w_bf16(win, w_in, 8, 2560, piece=512)
        load_w_bf16(wout, w_out, 8, 1024, piece=512)
        S.dve(lambda e: e.tensor_scalar(out=win.t[:, :, 1536:2048], in0=win.t[:, :, 1536:2048], scalar1=0.5, scalar2=None, op0=ALU.mult),
              reads=[win.r], writes=[win.r])
        for vsb in vs_rot.items:
            S.pool(lambda e, vsb=vsb: e.memset(vsb.t[:, :, :, 128:129], 1.0), writes=[vsb.r])
        S.pool(lambda e: e.memset(Vn.t[:, :, :, 128:129], 1.0), writes=[Vn.r])
        S.pool(lambda e: e.memset(QBD.t[:], 0.0), writes=[QBD.r])
        for qb_ in QB:
            S.pool(lambda e, qb_=qb_: e.memset(qb_.t[:], 0.0), writes=[qb_.r])

        def qkv_gen(tl, sample, qb=None):
            rows = tl.rows

            def proj(cbk, bk):
                for kc in range(8):
                    S.pe(lambda e, kc=kc: e.matmul(banks[bk][0:rows, :], lhsT=hT.t[:, kc, tl.coff:tl.coff + rows],
                                                   rhs=win.t[:, kc, cbk * 512:(cbk + 1) * 512], start=(kc == 0), stop=(kc == 7)),
                         reads=[hT.r, win.r], writes=[rb[bk]])
            proj(0, 6)
            proj(1, 7)
            rstd_groups(banks[6][0:rows, :], rb[6], rows, 8, 64, sq, ssq)
            S.dve(lambda e: e.tensor_tensor(out=t1.t[0:rows, :].rearrange("p (g d) -> p g d", d=64),
                                            in0=banks[6][0:rows, :].rearrange("p (g d) -> p g d", d=64),
                                            in1=ssq.t[0:rows, 8:16].unsqueeze(2).to_broadcast([rows, 8, 64]), op=ALU.mult),
                  reads=[rb[6], ssq.r], writes=[t1.r])
            S.dve(lambda e: e.tensor_tensor(out=qnb.t[0:rows, :].rearrange("p (g d) -> p g d", d=64),
                                            in0=t1.t[0:rows, :].rearrange("p (g d) -> p g d", d=64),
                                            in1=gq8.t[0:rows, :].unsqueeze(1).to_broadcast([rows, 8, 64]), op=ALU.mult),
                  reads=[t1.r, gq8.r], writes=[qnb.r])
            knf = knf_rot.next()
            rstd_groups(banks[7][0:rows, :], rb[7], rows, 8, 64, sq, ssk)
            S.dve(lambda e: e.tensor_tensor(out=t1.t[0:rows, :].rearrange("p (g d) -> p g d", d=64),
                                            in0=banks[7][0:rows, :].rearrange("p (g d) -> p g d", d=64),
                                            in1=ssk.t[0:rows, 8:16].unsqueeze(2).to_broadcast([rows, 8, 64]), op=ALU.mult),
                  reads=[rb[7], ssk.r], writes=[t1.r])
            S.dve(lambda e: e.tensor_tensor(out=knf.t[0:rows, :].rearrange("p (g d) -> p g d", d=64),
                                            in0=t1.t[0:rows, :].rearrange("p (g d) -> p g d", d=64),
                                            in1=gkb.t[0:rows, :].unsqueeze(1).to_broadcast([rows, 8, 64]), op=ALU.mult),
                  reads=[t1.r, gkb.r], writes=[knf.r])
            kdst = kso[0:64, :] if sample else kp[tl.r0:tl.r0 + rows, :]
            S.dma(lambda e: e.dma_start(out=kdst, in_=knf.t[0:rows, :]), reads=[knf.r])
            S.pool(lambda e: e.tensor_copy(out=knb.t[0:rows, :], in_=knf.t[0:rows, :]), reads=[knf.r], writes=[knb.r])
            yield
            proj(2, 6)
            vf = vf_rot.next()
            S.act(lambda e: e.activation(out=vf.t[0:rows, :], in_=banks[6][0:rows, :], func=AF.Copy), reads=[rb[6]], writes=[vf.r])
            vdst = vso[0:64, :] if sample else vp[tl.r0:tl.r0 + rows, :]
            S.dma(lambda e: e.dma_start(out=vdst, in_=vf.t[0:rows, :]), reads=[vf.r])
            if not sample:
                S.pool(lambda e: e.tensor_copy(out=VA.t[:, tl.idx, :, 0:128], in_=vf.t[:, :].rearrange("p (h e) -> p h e", e=128)),
                       reads=[vf.r], writes=[r_va[tl.idx]])
            pb = banks[7][:].bitcast(BF16)
            for h in range(4):
                S.pe(lambda e, h=h: e.transpose(out=pb[:, h * 128:h * 128 + rows], in_=qnb.t[0:rows, h * 128:(h + 1) * 128],
                                                identity=identb.t[0:rows, 0:rows]), reads=[qnb.r, identb.r], writes=[rb[7]])
            for h in range(4):
                S.pe(lambda e, h=h: e.transpose(out=pb[:, 512 + h * 128:512 + h * 128 + rows], in_=knb.t[0:rows, h * 128:(h + 1) * 128],
                                                identity=identb.t[0:rows, 0:rows]), reads=[knb.r, identb.r], writes=[rb[7]])
            qv = pb[:, 0:512].rearrange("p (h k) -> p h k", k=128)
            if sample:
                S.act(lambda e: e.activation(out=QT.t[:, :, tl.coff:tl.coff + rows], in_=qv[:, :, 0:rows], func=AF.Copy),
                      reads=[rb[7]], writes=[QT.r])
                S.dve(lambda e: e.tensor_copy(out=KnT.t[:, :, 0:rows],
                                              in_=pb[:, 512:1024].rearrange("p (h k) -> p h k", k=128)[:, :, 0:rows]),
                      reads=[rb[7]], writes=[KnT.r])
            else:
                S.act(lambda e: e.activation(out=qb.t[0:64, :, 0, tl.coff:tl.coff + rows], in_=qv[0:64, :, 0:rows], func=AF.Copy),
                      reads=[rb[7]], writes=[qb.r])
                S.act(lambda e: e.activation(out=qb.t[64:128, :, 1, tl.coff:tl.coff + rows], in_=qv[64:128, :, 0:rows], func=AF.Copy),
                      reads=[rb[7]], writes=[qb.r])
                S.dve(lambda e: e.tensor_copy(out=KT.t[:, tl.idx, :, :], in_=pb[:, 512:1024].rearrange("p (h k) -> p h k", k=128)),
                      reads=[rb[7]], writes=[r_kt[tl.idx]])
            yield

        def glu_gen(n, nseq, L, ch):
            hv = ch.t[:, :, 0:nseq * (30 + L)].rearrange("p c (s l) -> p c s l", s=nseq)
            for cb in range(4):
                pa, pg = 6, 7
                for which, bk in ((0, pa), (1, pg)):
                    c0 = 1536 + which * 512 + cb * 128
                    for kc in range(8):
                        S.pe(lambda e, kc=kc, c0=c0, bk=bk: e.matmul(banks[bk][:, 0:n], lhsT=win.t[:, kc, c0:c0 + 128], rhs=hT.t[:, kc, 0:n],
                                                                     start=(kc == 0), stop=(kc == 7)),
                             reads=[win.r, hT.r], writes=[rb[bk]])
                th = thg_rot.next()
                S.act(lambda e, th=th, pg=pg: e.activation(out=th.t[:, 0:n], in_=banks[pg][:, 0:n], func=AF.Tanh, scale=0.5),
                      reads=[rb[pg]], writes=[th.r])
                S.dve(lambda e, th=th, pa=pa, cb=cb: e.scalar_tensor_tensor(
                    out=hv[:, cb, :, 30:30 + L], in0=th.t[:, 0:n].rearrange("p (s l) -> p s l", s=nseq), scalar=1.0,
                    in1=banks[pa][:, 0:n].rearrange("p (s l) -> p s l", s=nseq), op0=ALU.add, op1=ALU.mult),
                    reads=[th.r, rb[pa]], writes=[ch.r])
                yield

        def stage_qkv(tl, sample, qb=None):
            run(qkv_gen(tl, sample, qb))

        def stage_glu(n, nseq, L, ch=None):
            run(glu_gen(n, nseq, L, ch or chist))

        def conv_ln_gen(n, nseq, L, ch=None):
            ch = ch or chist
            hv = ch.t[:, :, 0:nseq * (30 + L)].rearrange("p c (s l) -> p c s l", s=nseq)
            avs = [acc.t[:, cb, 0:n].rearrange("p (s l) -> p s l", s=nseq) for cb in range(4)]
            for cb in range(4):
                S.dve(lambda e, cb=cb: e.tensor_scalar(out=avs[cb], in0=hv[:, cb, :, 0:L], scalar1=wconv.t[:, cb, 0:1],
                                                       scalar2=cvec.t[:, 0, cb:cb + 1], op0=ALU.mult, op1=ALU.add),
                      reads=[ch.r, wconv.r, cvec.r], writes=[acc_r[cb]])
            yield
            for j in range(1, 31):
                for cb in range(4):
                    S.dve(lambda e, cb=cb, j=j: e.scalar_tensor_tensor(out=avs[cb], in0=hv[:, cb, :, j:j + L], scalar=wconv.t[:, cb, j:j + 1],
                                                                       in1=avs[cb], op0=ALU.mult, op1=ALU.add),
                          reads=[ch.r, wconv.r, acc_r[cb]], writes=[acc_r[cb]])
                yield
            S.act(lambda e: e.activation(out=accb.t[:, :, 0:n], in_=acc.t[:, :, 0:n], func=AF.Copy), reads=acc_r, writes=[accb.r])
            S.act(lambda e: e.activation(out=sqb.t[:, :, 0:n], in_=acc.t[:, :, 0:n], func=AF.Square), reads=acc_r, writes=[sqb.r])
            yield
            for cb in range(4):
                S.pe(lambda e, cb=cb: e.matmul(banks[6][:, 0:n], lhsT=onesb.t[:, :], rhs=accb.t[:, cb, 0:n], start=(cb == 0), stop=(cb == 3)),
                     reads=[onesb.r, accb.r], writes=[rb[6]])
            for cb in range(4):
                S.pe(lambda e, cb=cb: e.matmul(banks[7][:, 0:n], lhsT=onesb.t[:, :], rhs=sqb.t[:, cb, 0:n], start=(cb == 0), stop=(cb == 3)),
                     reads=[onesb.r, sqb.r], writes=[rb[7]])
            S.dve(lambda e: e.tensor_scalar(out=mean.t[:, 0:n], in0=banks[6][:, 0:n], scalar1=1.0 / 512, scalar2=None, op0=ALU.mult),
                  reads=[rb[6]], writes=[mean.r])
            S.dve(lambda e: e.tensor_tensor(out=rstd.t[:, 0:n], in0=mean.t[:, 0:n], in1=mean.t[:, 0:n], op=ALU.mult),
                  reads=[mean.r], writes=[rstd.r])
            S.dve(lambda e: e.scalar_tensor_tensor(out=rstd.t[:, 0:n], in0=banks[7][:, 0:n], scalar=1.0 / 512, in1=rstd.t[:, 0:n],
                                                   op0=ALU.mult, op1=ALU.subtract), reads=[rb[7], rstd.r], writes=[rstd.r])
            S.dve(lambda e: e.tensor_scalar(out=rstd.t[:, 0:n], in0=rstd.t[:, 0:n], scalar1=EPS, scalar2=None, op0=ALU.add),
                  reads=[rstd.r], writes=[rstd.r])
            S.act(lambda e: e.activation(out=rstd.t[:, 0:n], in_=rstd.t[:, 0:n], func=AF.Ln), reads=[rstd.r], writes=[rstd.r])
            S.act(lambda e: e.activation(out=rstd.t[:, 0:n], in_=rstd.t[:, 0:n], func=AF.Exp, scale=-0.5), reads=[rstd.r], writes=[rstd.r])
            yield
            for cb in range(4):
                y = y_rot.next(); th = th_rot.next()
                S.pool(lambda e, cb=cb, y=y: e.tensor_tensor(out=y.t[:, 0:n], in0=acc.t[:, cb, 0:n], in1=mean.t[:, 0:n], op=ALU.subtract),
                       reads=[acc_r[cb], mean.r], writes=[y.r])
                S.pool(lambda e, y=y: e.tensor_tensor(out=y.t[:, 0:n], in0=y.t[:, 0:n], in1=rstd.t[:, 0:n], op=ALU.mult),
                       reads=[y.r, rstd.r], writes=[y.r])
                S.dve(lambda e, cb=cb, y=y: e.tensor_scalar(out=y.t[:, 0:n], in0=y.t[:, 0:n], scalar1=cvec.t[:, 1, cb:cb + 1],
                                                            scalar2=cvec.t[:, 2, cb:cb + 1], op0=ALU.mult, op1=ALU.add),
                      reads=[y.r, cvec.r], writes=[y.r])
                S.act(lambda e, y=y, th=th: e.activation(out=th.t[:, 0:n], in_=y.t[:, 0:n], func=AF.Tanh), reads=[y.r], writes=[th.r])
                S.dve(lambda e, cb=cb, y=y, th=th: e.scalar_tensor_tensor(out=mixT.t[:, 4 + cb, 0:n], in0=th.t[:, 0:n], scalar=1.0, in1=y.t[:, 0:n],
                                                                          op0=ALU.add, op1=ALU.mult),
                      reads=[th.r, y.r], writes=[mixT_r[4 + cb]])
                yield

        def stage_conv_ln(n, nseq, L):
            run(conv_ln_gen(n, nseq, L))

        def conv_out(dst_ap, src_view):
            for cb in range(4):
                S.pe(lambda e, cb=cb: e.transpose(out=banks[2][0:30, cb * 128:(cb + 1) * 128], in_=src_view(cb), identity=identf.t[:, :]),
                     reads=[chist.r, identf.r], writes=[rb[2]])
            S.act(lambda e: e.activation(out=ctmp.t[:, :], in_=banks[2][0:30, :], func=AF.Copy), reads=[rb[2]], writes=[ctmp.r])
            S.dma(lambda e: e.dma_start(out=dst_ap, in_=ctmp.t[:, :]), reads=[ctmp.r])

        def epilogue(obank, rows, otok, h, qoff=0):
            rl = ep_rl.next(); o0 = o0_rot.next(); o = o_rot.next()
            ob = banks[obank]
            S.dve(lambda e: e.reciprocal(out=rl.t[0:rows, 0:1], in_=ob[0:rows, 128:129]), reads=[rb[obank]], writes=[rl.r])
            S.dve(lambda e: e.reciprocal(out=rl.t[0:rows, 1:2], in_=ob[0:rows, 260:261]), reads=[rb[obank]], writes=[rl.r])
            S.dve(lambda e: e.tensor_scalar(out=rl.t[0:rows, 2:3], in0=rl.t[0:rows, 1:2], scalar1=lamt.t[0:rows, 5:6], scalar2=None, op0=ALU.mult),
                  reads=[rl.r, lamt.r], writes=[rl.r])
            S.dve(lambda e: e.tensor_scalar(out=o0.t[0:rows, :], in0=ob[0:rows, 0:128], scalar1=rl.t[0:rows, 0:1], scalar2=None, op0=ALU.mult),
                  reads=[rb[obank], rl.r], writes=[o0.r])
            S.dve(lambda e: e.scalar_tensor_tensor(out=o.t[0:rows, :], in0=ob[0:rows, 132:260], scalar=rl.t[0:rows, 2:3], in1=o0.t[0:rows, :],
                                                   op0=ALU.mult, op1=ALU.add), reads=[rb[obank], rl.r, o0.r], writes=[o.r])
            S.pool(lambda e: e.tensor_tensor(out=o0.t[0:rows, :], in0=o.t[0:rows, :], in1=o.t[0:rows, :], op=ALU.mult), reads=[o.r], writes=[o0.r])
            S.dve(lambda e: e.reduce_sum(out=rl.t[0:rows, 3:4], in_=o0.t[0:rows, :], axis=AX.X), reads=[o0.r], writes=[rl.r])
            S.dve(lambda e: e.tensor_scalar(out=rl.t[0:rows, 3:4], in0=rl.t[0:rows, 3:4], scalar1=1.0 / 128, scalar2=EPS, op0=ALU.mult, op1=ALU.add),
                  reads=[rl.r], writes=[rl.r])
            S.pool(lambda e: e.tensor_tensor(out=rl.t[0:rows, 4:5], in0=rl.t[0:rows, 3:4], in1=mhalf.t[0:rows, 0:1], op=ALU.pow),
                   reads=[rl.r, mhalf.r], writes=[rl.r])
            S.dve(lambda e: e.scalar_tensor_tensor(out=otok.t[0:rows, h, :], in0=o.t[0:rows, :], scalar=rl.t[0:rows, 4:5], in1=gsubb.t[0:rows, :],
                                                   op0=ALU.mult, op1=ALU.mult), reads=[o.r, rl.r, gsubb.r], writes=[otok.r])

        def otok_to_mixT(otok, rows, coff):
            pb = banks[0][:].bitcast(BF16)
            for h in range(4):
                S.pe(lambda e, h=h: e.transpose(out=pb[:, h * 128:h * 128 + rows], in_=otok.t[0:rows, h, :], identity=identb.t[0:rows, 0:rows]),
                     reads=[otok.r, identb.r], writes=[rb[0]])
            S.act(lambda e: e.activation(out=mixT.t[:, 0:4, coff:coff + rows],
                                         in_=pb[:, 0:512].rearrange("p (h k) -> p h k", k=128)[:, :, 0:rows], func=AF.Copy),
                  reads=[rb[0]], writes=mixT_r[0:4])

        def stage_wout(tl, src_ap, dst_ap, dst_res):
            rows = tl.rows
            xr = xin_rot.next()
            S.dma(lambda e: e.dma_start(out=xr.t[0:rows, :], in_=src_ap), writes=[xr.r])
            for half in range(2):
                bk = half
                for c in range(8):
                    S.pe(lambda e, c=c, half=half, bk=bk: e.matmul(banks[bk][0:rows, :], lhsT=mixT.t[:, c, tl.coff:tl.coff + rows],
                                                                   rhs=wout.t[:, c, half * 512:(half + 1) * 512], start=(c == 0), stop=(c == 7)),
                         reads=[mixT_r[c], wout.r], writes=[rb[bk]])
                S.dve(lambda e, half=half, bk=bk: e.tensor_tensor(out=xr.t[0:rows, half * 512:(half + 1) * 512], in0=xr.t[0:rows, half * 512:(half + 1) * 512],
                                                                  in1=banks[bk][0:rows, :], op=ALU.add), reads=[xr.r, rb[bk]], writes=[xr.r])
            S.dma(lambda e: e.dma_start(out=dst_ap, in_=xr.t[0:rows, :]), reads=[xr.r], writes=[dst_res])

        stl = TT(NT, 64, SEQ, 0)
        stage_norm([stl], lambda tl: xs[0:64, :], lambda tl: [], 0, hT, xin_rot, hn_rot, ss_rot, junk, 3)
        stage_qkv(stl, True)
        for s in range(4):
            for m in range(2):
                S.pool(lambda e, s=s, m=m: e.tensor_copy(out=QBD.t[m * 64:(m + 1) * 64, s, :, m * 16:(m + 1) * 16],
                                                         in_=QT.t[m * 64:(m + 1) * 64, :, s * 16:(s + 1) * 16]),
                       reads=[QT.r], writes=[QBD.r])
        for s in range(4):
            for kc in range(8):
                S.pe(lambda e, s=s, kc=kc: e.matmul(banks[2][0:16, :], lhsT=hT.t[:, kc, s * 16:(s + 1) * 16], rhs=win.t[:, kc, 1024:1536],
                                                    start=(kc == 0), stop=(kc == 7)), reads=[hT.r, win.r], writes=[rb[2]])
            S.act(lambda e, s=s: e.activation(out=Vn.t[0:16, s, :, 0:128], in_=banks[2][0:16, :].rearrange("p (h e) -> p h e", e=128), func=AF.Copy),
                  reads=[rb[2]], writes=[Vn.r])
        hvs = chist.t[:, :, 0:4 * 46].rearrange("p c (s l) -> p c s l", s=4)
        for s in range(4):
            S.dma(lambda e, s=s: e.dma_start(out=ctmp.t[:, :], in_=cc[s]), writes=[ctmp.r])
            for cb in range(4):
                S.pe(lambda e, s=s, cb=cb: e.transpose(out=banks[2][:, cb * 32:cb * 32 + 30], in_=ctmp.t[0:30, cb * 128:(cb + 1) * 128],
                                                       identity=identf.t[0:30, 0:30]), reads=[ctmp.r, identf.r], writes=[rb[2]])
            S.act(lambda e, s=s: e.activation(out=hvs[:, :, s, 0:30], in_=banks[2][:, 0:128].rearrange("p (c l) -> p c l", l=32)[:, :, 0:30],
                                              func=AF.Copy), reads=[rb[2]], writes=[chist.r])
        stage_glu(64, 4, 16)
        for s in range(4):
            conv_out(cso[s], lambda cb, s=s: hvs[:, cb, s, 16:46])
        stage_conv_ln(64, 4, 16)

        sb_otok = otok_rot.next()
        for s in range(4):
            obanks = (5, 6, 7)

            def oacc(a):
                return banks[5 + a // 3][0:16, (a % 3) * 132:(a % 3) * 132 + 129], 5 + a // 3
            first_in_bank = {5: True, 6: True, 7: True}
            nchunks = 8
            for c in range(nchunks + 1):
                last = (c == nchunks)
                if not last:
                    kraw = kraw_rot.next(); vraw = vraw_rot.next(); vsb = vs_rot.next(); ktc = ktc_rot.next(); pts = pts_rot.next()
                    S.dma(lambda e, kraw=kraw, c=c, s=s: e.dma_start(out=kraw.t[:, :, :], in_=ck[s, c * 512:(c + 1) * 512, :].rearrange("(t p) c -> p t c", p=128)),
                          writes=[kraw.r])
                    S.dma(lambda e, vraw=vraw, c=c, s=s: e.dma_start(out=vraw.t[:, :, :], in_=cv[s, c * 512:(c + 1) * 512, :].rearrange("(t p) c -> p t c", p=128)),
                          writes=[vraw.r])
                    tbk = (0, 1, 3, 4)
                    for tt in range(4):
                        bk = tbk[tt]
                        for h in range(4):
                            S.pe(lambda e, tt=tt, h=h, kraw=kraw, bk=bk: e.transpose(
                                out=banks[bk][:, h * 128:(h + 1) * 128], in_=kraw.t[:, tt, h * 128:(h + 1) * 128],
                                identity=identf.t[:, :]), reads=[kraw.r, identf.r], writes=[rb[bk]])
                        if tt % 2 == 0:
                            S.act(lambda e, tt=tt, ktc=ktc, bk=bk: e.activation(out=ktc.t[:, tt, :, :].rearrange("p h k -> p (h k)"),
                                                                                in_=banks[bk][:, :], func=AF.Copy), reads=[rb[bk]], writes=[ktc.r])
                            S.pool(lambda e, tt=tt, vsb=vsb, vraw=vraw: e.tensor_copy(out=vsb.t[:, tt, :, 0:128],
                                                                                      in_=vraw.t[:, tt, :].rearrange("p (h e) -> p h e", e=128)),
                                   reads=[vraw.r], writes=[vsb.r])
                        else:
                            S.dve(lambda e, tt=tt, ktc=ktc, bk=bk: e.tensor_copy(out=ktc.t[:, tt, :, :].rearrange("p h k -> p (h k)"),
                                                                                 in_=banks[bk][:, :]), reads=[rb[bk]], writes=[ktc.r])
                            S.act(lambda e, tt=tt, vsb=vsb, vraw=vraw: e.activation(out=vsb.t[:, tt, :, 0:128],
                                                                                    in_=vraw.t[:, tt, :].rearrange("p (h e) -> p h e", e=128), func=AF.Copy),
                                  reads=[vraw.r], writes=[vsb.r])
                    ntl, krows = 4, 128
                    for tt in range(4):
                        for h in range(4):
                            S.pe(lambda e, tt=tt, h=h, ktc=ktc, s=s: e.matmul(banks[2][:, tt * 128 + h * 32:tt * 128 + (h + 1) * 32], lhsT=ktc.t[:, tt, h, :],
                                                                             rhs=QBD.t[:, s, h, :], start=True, stop=True, skip_group_check=True),
                                 reads=[ktc.r, QBD.r], writes=[rb[2]])
                    if c == nchunks - 1:
                        S.dve(lambda e: e.tensor_tensor(out=nears.t[:, :].rearrange("p (h m q) -> p h m q", h=4, m=2),
                                                        in0=banks[2][:, 384:512].rearrange("p (h m q) -> p h m q", h=4, m=2),
                                                        in1=biasT.t[:, 1, :, 0:16].unsqueeze(2).to_broadcast([128, 4, 2, 16]), op=ALU.add),
                              reads=[rb[2], biasT.r], writes=[nears.r])
                        S.act(lambda e, pts=pts: e.activation(out=pts.t[:, 0:3, :], in_=banks[2][:, 0:384].rearrange("p (t c) -> p t c", c=128), func=AF.Exp),
                              reads=[rb[2]], writes=[pts.r])
                        S.act(lambda e, pts=pts: e.activation(out=pts.t[:, 3, :], in_=nears.t[:, :], func=AF.Exp), reads=[nears.r], writes=[pts.r])
                    else:
                        S.act(lambda e, pts=pts: e.activation(out=pts.t[:, :, :], in_=banks[2][:, :].rearrange("p (t c) -> p t c", c=128), func=AF.Exp),
                              reads=[rb[2]], writes=[pts.r])
                    vsrc = lambda tt, h, vsb=vsb: vsb.t[:, tt, h, 0:129]
                    vres = vsb.r
                else:
                    pts = pts_rot.next()
                    ntl, krows = 1, 16
                    for h in range(4):
                        S.pe(lambda e, h=h, s=s: e.matmul(banks[2][0:16, h * 32:(h + 1) * 32], lhsT=KnT.t[:, h, s * 16:(s + 1) * 16], rhs=QBD.t[:, s, h, :],
                                                          start=True, stop=True, skip_group_check=True), reads=[KnT.r, QBD.r], writes=[rb[2]])
                    S.dve(lambda e: e.tensor_tensor(out=nears.t[0:16, :].rearrange("p (h m q) -> p h m q", h=4, m=2),
                                                    in0=banks[2][0:16, 0:128].rearrange("p (h m q) -> p h m q", h=4, m=2),
                                                    in1=biasT.t[0:16, 0, :, 0:16].unsqueeze(2).to_broadcast([16, 4, 2, 16]), op=ALU.add),
                          reads=[rb[2], biasT.r], writes=[nears.r])
                    S.act(lambda e, pts=pts: e.activation(out=pts.t[0:16, 0, :], in_=nears.t[0:16, :], func=AF.Exp), reads=[nears.r], writes=[pts.r])
                    vsrc = lambda tt, h, s=s: Vn.t[0:16, s, h, 0:129]
                    vres = Vn.r
                for tt in range(ntl):
                    for a in range(8):
                        oap, obk = oacc(a)
                        st = first_in_bank[obk]
                        first_in_bank[obk] = False
                        S.pe(lambda e, tt=tt, a=a, oap=oap, st=st, pts=pts, vsrc=vsrc, krows=krows, last=last: e.matmul(
                            oap, lhsT=pts.t[0:krows, tt, a * 16:(a + 1) * 16], rhs=vsrc(tt, a // 2), start=st, stop=(last),
                            skip_group_check=True), reads=[pts.r, vres], writes=[rb[obk]])
            for h in range(4):
                rl = ep_rl.next(); o0 = o0_rot.next(); o = o_rot.next()
                a0, a1 = 2 * h, 2 * h + 1
                p0, bk0 = oacc(a0); p1, bk1 = oacc(a1)
                S.dve(lambda e, p0=p0, rl=rl: e.reciprocal(out=rl.t[0:16, 0:1], in_=p0[:, 128:129]), reads=[rb[bk0]], writes=[rl.r])
                S.dve(lambda e, p1=p1, rl=rl: e.reciprocal(out=rl.t[0:16, 1:2], in_=p1[:, 128:129]), reads=[rb[bk1]], writes=[rl.r])
                S.dve(lambda e, rl=rl: e.tensor_scalar(out=rl.t[0:16, 2:3], in0=rl.t[0:16, 1:2], scalar1=lamt.t[0:16, 5:6], scalar2=None, op0=ALU.mult),
                      reads=[rl.r, lamt.r], writes=[rl.r])
                S.act(lambda e, p0=p0, rl=rl, o0=o0: e.activation(out=o0.t[0:16, :], in_=p0[:, 0:128], func=AF.Copy, scale=rl.t[0:16, 0:1]),
                      reads=[rb[bk0], rl.r], writes=[o0.r])
                S.dve(lambda e, p1=p1, rl=rl, o0=o0, o=o: e.scalar_tensor_tensor(out=o.t[0:16, :], in0=p1[:, 0:128], scalar=rl.t[0:16, 2:3], in1=o0.t[0:16, :],
                                                                                 op0=ALU.mult, op1=ALU.add), reads=[rb[bk1], rl.r, o0.r], writes=[o.r])
                S.act(lambda e, rl=rl, o=o: e.activation(out=ojunk.t[0:16, :], in_=o.t[0:16, :], func=AF.Square, accum_out=rl.t[0:16, 3:4]),
                      reads=[o.r], writes=[ojunk.r, rl.r])
                S.dve(lambda e, rl=rl: e.tensor_scalar(out=rl.t[0:16, 3:4], in0=rl.t[0:16, 3:4], scalar1=1.0 / 128, scalar2=EPS, op0=ALU.mult, op1=ALU.add),
                      reads=[rl.r], writes=[rl.r])
                S.pool(lambda e, rl=rl: e.tensor_tensor(out=rl.t[0:16, 4:5], in0=rl.t[0:16, 3:4], in1=mhalf.t[0:16, 0:1], op=ALU.pow),
                       reads=[rl.r, mhalf.r], writes=[rl.r])
                S.dve(lambda e, rl=rl, o=o, h=h: e.scalar_tensor_tensor(out=sb_otok.t[0:16, h, :], in0=o.t[0:16, :], scalar=rl.t[0:16, 4:5],
                                                                        in1=gsubb.t[0:16, :], op0=ALU.mult, op1=ALU.mult),
                      reads=[o.r, rl.r, gsubb.r], writes=[sb_otok.r])
            otok_to_mixT(sb_otok, 16, s * 16)
            sb_otok = otok_rot.next()
        stage_wout(stl, xs[0:64, :], X1[SEQ:SEQ + 64, :], r_x1[NT])

        S.barrier()
        S.pool(lambda e: e.memset(VA.t[:, :, :, 128:129], 1.0), writes=r_va)
        pcg = precast_gen()
        NG = SEQ // GA
        tpg = GA // 128

        def attn_gen(g, otoks, qb):
            i0 = g * tpg
            nkt = i0 + tpg
            for h in range(4):
                ob = [2 + (h % 2) * 2 + i for i in range(tpg)]
                started = [False] * tpg
                sbuf_i = [0]

                def emit_scores(j, h=h):
                    sb = sbuf_i[0] % 2
                    sbuf_i[0] += 1
                    il0 = max(0, j - i0)
                    c0 = il0 * 128
                    sv = banks[sb][:, :].rearrange("p (m q) -> p m q", m=2)
                    S.pe(lambda e, sb=sb, j=j: e.matmul(banks[sb][:, :], lhsT=KT.t[:, j, h, :], rhs=qb.t[:, h, :, :].rearrange("p m q -> p (m q)"),
                                                        start=True, stop=True, skip_group_check=True),
                         reads=[r_kt[j], qb.r], writes=[rb[sb]])
                    for il in range(il0, tpg):
                        dlt = (i0 + il) - j
                        if dlt >= 2:
                            break
                        for m in range(2):
                            for part in range(2):
                                S.pe(lambda e, sv=sv, m=m, part=part, il=il, dlt=dlt: e.matmul(
                                    sv[:, m, il * 128:(il + 1) * 128], lhsT=identb.t[:, :], rhs=biasHL.t[:, part, dlt, h, :],
                                    start=False, stop=True, skip_group_check=True), reads=[identb.r, biasHL.r], writes=[rb[sb]])
                    return sb, il0

                pend = emit_scores(0)
                for j in range(nkt):
                    sb, il0 = pend
                    if j + 1 < nkt:
                        pend = emit_scores(j + 1)
                    pt = PT_rot.next()
                    sv = banks[sb][:, :].rearrange("p (m q) -> p m q", m=2)
                    S.act(lambda e, c0=il0 * 128, sv=sv, pt=pt: e.activation(out=pt.t[:, :, c0:GA], in_=sv[:, :, c0:GA], func=AF.Exp),
                          reads=[rb[sb]], writes=[pt.r])
                    for _ in range(3):
                        S.pe(lambda e: e.ldweights(identb.t[:, :]), reads=[identb.r])
                    for il in range(il0, tpg):
                        lastk = (j == i0 + il)
                        for m in range(2):
                            st = not started[il]
                            started[il] = True
                            S.pe(lambda e, m=m, il=il, pt=pt, st=st, lastk=lastk, j=j: e.matmul(
                                banks[ob[il]][:, m * 132:m * 132 + 129], lhsT=pt.t[:, m, il * 128:(il + 1) * 128], rhs=VA.t[:, j, h, 0:129],
                                start=st, stop=lastk, skip_group_check=True), reads=[pt.r, r_va[j]], writes=[rb[ob[il]]])
                        if lastk:
                            epilogue(ob[il], 128, otoks[il], h)
                    yield

        def gtiles(g):
            return [TT(g * tpg + i, 128, (g * tpg + i) * 128, i * 128) for i in range(tpg)]

        chs = [chist, chist2]

        def s1_gen(g):
            ch = chs[g % 2]
            yield from norm_gen(gtiles(g), lambda tl: xp[tl.r0:tl.r0 + 128, :], lambda tl: [], 0, hT, xin_rot, hn_rot, ss_rot, junk, 6)
            for tl in gtiles(g):
                yield from qkv_gen(tl, False, QB[g % 2])
            yield from glu_gen(GA, 1, GA, ch)
            if g == 0:
                S.pool(lambda e: e.memset(ch.t[:, :, 0:30], 0.0), reads=[ch.r], writes=[ch.r])
            else:
                pch = chs[(g - 1) % 2]
                S.pool(lambda e: e.tensor_copy(out=ch.t[:, :, 0:30], in_=pch.t[:, :, GA:GA + 30]), reads=[pch.r, ch.r], writes=[ch.r])
            yield

        run(s1_gen(0))
        for g in range(NG):
            tiles = gtiles(g)
            ch = chs[g % 2]
            if g == NG - 1:
                conv_out(cpo[:, :], lambda cb: ch.t[:, cb, GA:GA + 30])
            otoks = [otok_rot.next() for _ in range(tpg)]
            step(pcg)
            ag = attn_gen(g, otoks, QB[g % 2])
            cg = conv_ln_gen(GA, 1, GA, ch)
            ng = s1_gen(g + 1) if g + 1 < NG else iter(())
            nsteps = 4 * (g * tpg + tpg)
            per_c = max(1, -(-38 // nsteps))
            n_s1 = 2 * tpg + 2 * tpg + 4 + 1
            per_n = max(1, -(-n_s1 // nsteps))
            every_n = max(1, nsteps // n_s1)
            k = 0
            while step(ag):
                k += 1
                step(cg, per_c)
                if k % every_n == 0:
                    step(ng, per_n)
            run(cg)
            for il in range(tpg):
                otok_to_mixT(otoks[il], 128, il * 128)
            for tl in tiles:
                stage_wout(tl, xp[tl.r0:tl.r0 + 128, :], X1[tl.r0:tl.r0 + 128, :], r_x1[tl.idx])
            run(ng)

    run(pcg)
    S.barrier()
    GB = 512
    with ExitStack() as stB:
        def BB(name, shape, dtype):
            return B(name, shape, dtype, stB)
        wff1 = BB("wff1", [128, 8, 4096], BF16)
        xinB = Rot([BB(f"xinB{i}", [128, D], F32) for i in range(2)])
        hnB = Rot([BB("hnB0", [128, D], BF16)])
        ssB = Rot([BB(f"ssB{i}", [128, 2], F32) for i in range(2)])
        junkB = BB("junkB", [128, D], BF16)
        hTB = BB("hTB", [128, 8, GB], BF16)
        xoB = Rot([BB(f"xoB{i}", [128, D], F32) for i in range(2)])

        with ExitStack() as stB1:
            def B1(name, shape, dtype):
                return B(name, shape, dtype, stB1)
            wmq = B1("wmq", [128, 8, 512], BF16); wmo = B1("wmo", [128, 4, 1024], BF16)
            wmk = B1("wmk", [128, 8, 512], BF16); wmv = B1("wmv", [128, 8, 512], BF16)
            MKT = B1("MKT", [128, 5, 4, 256], BF16)
            MV = B1("MV", [128, 5, 2, 4, 132], BF16)
            mT = B1("mT", [128, 8, 256], BF16)
            sqm = B1("sqm", [128, 512], F32); ssm = B1("ssm", [128, 8], F32)
            t1m = B1("t1m", [128, 512], F32)
            mkf = Rot([B1(f"mkf{i}", [128, 512], F32) for i in range(2)])
            mkb = B1("mkb", [128, 512], BF16)
            mvf = Rot([B1(f"mvf{i}", [128, 512], F32) for i in range(2)])
            cmkb = B1("cmkb", [128, 2, 512], BF16)
            qmb = B1("qmb", [128, 512], BF16)
            PTm = Rot([B1(f"PTm{i}", [128, 2, GB], BF16) for i in range(2)])
            rlm = Rot([B1(f"rlm{i}", [128, 4], F32) for i in range(2)])

            load_wb(wmk, "wmk", 8)
            load_wb(wmv, "wmv", 8)
            load_wb(wmq, "wmq", 8)
            load_wb(wmo, "wmo", 4)
            load_wb(wff1, "wff1", 8, piece=1024)
            S.pool(lambda e: e.memset(MV.t[:, :, :, :, 128:129], 1.0), writes=[MV.r])

            mtiles = [TT(i, 128, i * 128, i * 128) for i in range(2)]
            stage_norm(mtiles, lambda tl: memp[tl.r0:tl.r0 + 128, :], lambda tl: [], 3, mT, xinB, hnB, ssB, junkB, 3)
            for tl in mtiles:
                for kc in range(8):
                    for which, wt in ((0, wmk), (1, wmv)):
                        S.pe(lambda e, kc=kc, which=which, wt=wt, tl=tl: e.matmul(banks[which][:, :], lhsT=mT.t[:, kc, tl.coff:tl.coff + 128], rhs=wt.t[:, kc, :],
                                                                                  start=(kc == 0), stop=(kc == 7)), reads=[mT.r, wt.r], writes=[rb[which]])
                rstd_groups(banks[0][:, :], rb[0], 128, 4, 128, sqm, ssm)
                kf = mkf.next(); vf = mvf.next()
                S.dve(lambda e: e.tensor_tensor(out=t1m.t[:, :].rearrange("p (g d) -> p g d", d=128), in0=banks[0][:, :].rearrange("p (g d) -> p g d", d=128),
                                                in1=ssm.t[:, 4:8].unsqueeze(2).to_broadcast([128, 4, 128]), op=ALU.mult), reads=[rb[0], ssm.r], writes=[t1m.r])
                S.dve(lambda e, kf=kf: e.tensor_tensor(out=kf.t[:, :].rearrange("p (g d) -> p g d", d=128), in0=t1m.t[:, :].rearrange("p (g d) -> p g d", d=128),
                                                       in1=gmkb.t[:, :].unsqueeze(1).to_broadcast([128, 4, 128]), op=ALU.mult), reads=[t1m.r, gmkb.r], writes=[kf.r])
                S.dma(lambda e, kf=kf, tl=tl: e.dma_start(out=mkp[tl.r0:tl.r0 + 128, :], in_=kf.t[:, :]), reads=[kf.r])
                S.pool(lambda e, kf=kf: e.tensor_copy(out=mkb.t[:, :], in_=kf.t[:, :]), reads=[kf.r], writes=[mkb.r])
                S.act(lambda e, vf=vf: e.activation(out=vf.t[:, :], in_=banks[1][:, :], func=AF.Copy), reads=[rb[1]], writes=[vf.r])
                S.dma(lambda e, vf=vf, tl=tl: e.dma_start(out=mvp[tl.r0:tl.r0 + 128, :], in_=vf.t[:, :]), reads=[vf.r])
                S.pool(lambda e, vf=vf, tl=tl: e.tensor_copy(out=MV.t[:, 0, tl.idx, :, 0:128], in_=vf.t[:, :].rearrange("p (h e) -> p h e", e=128)),
                       reads=[vf.r], writes=[MV.r])
                pb = banks[2][:].bitcast(BF16)
                for h in range(4):
                    S.pe(lambda e, h=h: e.transpose(out=pb[:, h * 128:(h + 1) * 128], in_=mkb.t[:, h * 128:(h + 1) * 128], identity=identb.t[:, :]),
                         reads=[mkb.r, identb.r], writes=[rb[2]])
                S.act(lambda e, tl=tl: e.activation(out=MKT.t[:, 0, :, tl.coff:tl.coff + 128], in_=pb[:, 0:512].rearrange("p (h k) -> p h k", k=128), func=AF.Copy),
                      reads=[rb[2]], writes=[MKT.r])
            for s in range(4):
                S.dma(lambda e, s=s: e.dma_start(out=cmkb.t[:, :, :], in_=cmk[s].rearrange("(t p) c -> p t c", p=128)), writes=[cmkb.r], q="pool")
                for tt in range(2):
                    S.dma(lambda e, s=s, tt=tt: e.dma_start(out=MV.t[:, 1 + s, tt, :, 0:128],
                                                            in_=cmv[s, tt * 128:(tt + 1) * 128, :].rearrange("p (h e) -> p h e", e=128)),
                          writes=[MV.r], q="pool")
                for tt in range(2):
                    pb = banks[2 + tt][:].bitcast(BF16)
                    for h in range(4):
                        S.pe(lambda e, h=h, tt=tt, pb=pb: e.transpose(out=pb[:, h * 128:(h + 1) * 128], in_=cmkb.t[:, tt, h * 128:(h + 1) * 128], identity=identb.t[:, :]),
                             reads=[cmkb.r, identb.r], writes=[rb[2 + tt]])
                    S.act(lambda e, s=s, tt=tt, pb=pb: e.activation(out=MKT.t[:, 1 + s, :, tt * 128:(tt + 1) * 128],
                                                                    in_=pb[:, 0:512].rearrange("p (h k) -> p h k", k=128), func=AF.Copy),
                          reads=[rb[2 + tt]], writes=[MKT.r])

            def cross_s1(tiles, QmT):
                ng = norm_gen(tiles, lambda tl: X1[tl.r0:tl.r0 + tl.rows, :], lambda tl: [r_x1[tl.idx]], 1, hTB, xinB, hnB, ssB, junkB, 0)
                for tl in tiles:
                    rows = tl.rows
                    step(ng, 1)
                    yield
                    step(ng, 1)
                    yield
                    for kc in range(8):
                        S.pe(lambda e, kc=kc, tl=tl, rows=rows: e.matmul(banks[1][0:rows, :], lhsT=hTB.t[:, kc, tl.coff:tl.coff + rows], rhs=wmq.t[:, kc, :],
                                                                         start=(kc == 0), stop=(kc == 7)), reads=[hTB.r, wmq.r], writes=[rb[1]])
                    rstd_groups(banks[1][0:rows, :], rb[1], rows, 4, 128, sqm, ssm)
                    S.dve(lambda e, rows=rows: e.tensor_tensor(out=t1m.t[0:rows, :].rearrange("p (g d) -> p g d", d=128),
                                                               in0=banks[1][0:rows, :].rearrange("p (g d) -> p g d", d=128),
                                                               in1=ssm.t[0:rows, 4:8].unsqueeze(2).to_broadcast([rows, 4, 128]), op=ALU.mult),
                          reads=[rb[1], ssm.r], writes=[t1m.r])
                    S.dve(lambda e, rows=rows: e.tensor_tensor(out=qmb.t[0:rows, :].rearrange("p (g d) -> p g d", d=128),
                                                               in0=t1m.t[0:rows, :].rearrange("p (g d) -> p g d", d=128),
                                                               in1=gmqb.t[0:rows, :].unsqueeze(1).to_broadcast([rows, 4, 128]), op=ALU.mult),
                          reads=[t1m.r, gmqb.r], writes=[qmb.r])
                    yield
                    pb = banks[0][:].bitcast(BF16)
                    for h in range(4):
                        S.pe(lambda e, h=h, rows=rows, pb=pb: e.transpose(out=pb[:, h * 128:h * 128 + rows], in_=qmb.t[0:rows, h * 128:(h + 1) * 128],
                                                                         identity=identb.t[0:rows, 0:rows]), reads=[qmb.r, identb.r], writes=[rb[0]])
                    S.act(lambda e, tl=tl, rows=rows, pb=pb: e.activation(out=QmT.t[:, :, tl.coff:tl.coff + rows],
                                                                          in_=pb[:, 0:512].rearrange("p (h k) -> p h k", k=128)[:, :, 0:rows], func=AF.Copy),
                          reads=[rb[0]], writes=[QmT.r])
                    yield

            def cross_s2(seqs, QmT, omT, otoks):
                mems = {}
                for ui, u in enumerate(seqs):
                    mems.setdefault(u[0], []).append((ui, u))
                for wm, units in mems.items():
                    c_lo = min(u[1] for _, u in units); c_hi = max(u[1] + u[2] for _, u in units)
                    for h in range(4):
                        pt = PTm.next()
                        for mt in range(2):
                            S.pe(lambda e, h=h, mt=mt, wm=wm: e.matmul(banks[2 + mt][:, c_lo:c_hi], lhsT=MKT.t[:, wm, h, mt * 128:(mt + 1) * 128],
                                                                       rhs=QmT.t[:, h, c_lo:c_hi], start=True, stop=True),
                                 reads=[MKT.r, QmT.r], writes=[rb[2 + mt]])
                            S.act(lambda e, mt=mt, pt=pt: e.activation(out=pt.t[:, mt, c_lo:c_hi], in_=banks[2 + mt][:, c_lo:c_hi], func=AF.Exp),
                                  reads=[rb[2 + mt]], writes=[pt.r])
                        yield
                        for k, (ui, u) in enumerate(units):
                            _, c0, ncol = u
                            obk = 4 + (k % 2)
                            for mt in range(2):
                                S.pe(lambda e, mt=mt, pt=pt, c0=c0, ncol=ncol, obk=obk, wm=wm, h=h: e.matmul(
                                    banks[obk][0:ncol, 0:129], lhsT=pt.t[:, mt, c0:c0 + ncol], rhs=MV.t[:, wm, mt, h, 0:129], start=(mt == 0), stop=(mt == 1)),
                                    reads=[pt.r, MV.r], writes=[rb[obk]])
                            rl = rlm.next()
                            ot = otoks[ui]
                            S.dve(lambda e, rl=rl, obk=obk, ncol=ncol: e.reciprocal(out=rl.t[0:ncol, 0:1], in_=banks[obk][0:ncol, 128:129]),
                                  reads=[rb[obk]], writes=[rl.r])
                            S.act(lambda e, rl=rl, obk=obk, ncol=ncol, ot=ot, h=h: e.activation(out=ot.t[0:ncol, h, :], in_=banks[obk][0:ncol, 0:128], func=AF.Copy,
                                                                                                scale=rl.t[0:ncol, 0:1]), reads=[rb[obk], rl.r], writes=[ot.r])
                        yield
                    for ui, u in units:
                        _, c0, ncol = u
                        ot = otoks[ui]
                        pb = banks[0][:].bitcast(BF16)
                        for h in range(4):
                            S.pe(lambda e, h=h, ot=ot, ncol=ncol, pb=pb: e.transpose(out=pb[:, h * 128:h * 128 + ncol], in_=ot.t[0:ncol, h, :],
                                                                                    identity=identb.t[0:ncol, 0:ncol]), reads=[ot.r, identb.r], writes=[rb[0]])
                        S.act(lambda e, c0=c0, ncol=ncol, pb=pb: e.activation(out=omT.t[:, :, c0:c0 + ncol],
                                                                              in_=pb[:, 0:512].rearrange("p (h k) -> p h k", k=128)[:, :, 0:ncol], func=AF.Copy),
                              reads=[rb[0]], writes=[omT.r])
                        yield

            def cross_s3(tiles, omT):
                for tl in tiles:
                    rows = tl.rows
                    xr = xoB.next()
                    S.dma(lambda e, xr=xr, tl=tl, rows=rows: e.dma_start(out=xr.t[0:rows, :], in_=X1[tl.r0:tl.r0 + rows, :]), reads=[r_x1[tl.idx]], writes=[xr.r])
                    for half in range(2):
                        bk = 6 + half
                        for c in range(4):
                            S.pe(lambda e, c=c, half=half, bk=bk, tl=tl, rows=rows: e.matmul(banks[bk][0:rows, :], lhsT=omT.t[:, c, tl.coff:tl.coff + rows],
                                                                                            rhs=wmo.t[:, c, half * 512:(half + 1) * 512], start=(c == 0), stop=(c == 3)),
                                 reads=[omT.r, wmo.r], writes=[rb[bk]])
                        S.dve(lambda e, half=half, bk=bk, xr=xr, rows=rows: e.tensor_tensor(out=xr.t[0:rows, half * 512:(half + 1) * 512],
                                                                                            in0=xr.t[0:rows, half * 512:(half + 1) * 512], in1=banks[bk][0:rows, :], op=ALU.add),
                              reads=[xr.r, rb[bk]], writes=[xr.r])
                    S.dma(lambda e, xr=xr, tl=tl, rows=rows: e.dma_start(out=X2[tl.r0:tl.r0 + rows, :], in_=xr.t[0:rows, :]), reads=[xr.r], writes=[r_x2[tl.idx]])
                    yield

            tpgB = GB // 128
            QmTs = [B1(f"QmT{i}", [128, 4, GB], BF16) for i in range(2)]
            omTs = [B1(f"omT{i}", [128, 4, GB], BF16) for i in range(2)]
            pots = [[B1(f"pot{p}_{i}", [128, 4, 128], BF16) for i in range(tpgB)] for p in range(2)]
            cgroups = []
            stl = TT(NT, 64, SEQ, 0)
            cgroups.append(([stl], [(1 + sq_, sq_ * 16, 16) for sq_ in range(4)]))
            for g in range(SEQ // GB):
                tiles = [TT(g * tpgB + i, 128, (g * tpgB + i) * 128, i * 128) for i in range(tpgB)]
                cgroups.append((tiles, [(0, i * 128, 128) for i in range(tpgB)]))
            ngr = len(cgroups)
            for t in range(ngr + 2):
                gens = []
                if t < ngr:
                    gens.append(cross_s1(cgroups[t][0], QmTs[t % 2]))
                if 0 <= t - 1 < ngr:
                    gens.append(cross_s2(cgroups[t - 1][1], QmTs[(t - 1) % 2], omTs[(t - 1) % 2], pots[(t - 1) % 2]))
                if 0 <= t - 2 < ngr:
                    gens.append(cross_s3(cgroups[t - 2][0], omTs[(t - 2) % 2]))
                alive = list(gens)
                while alive:
                    alive = [gq_ for gq_ in alive if step(gq_)]

        S.barrier()
        with ExitStack() as stB2:
            def B2(name, shape, dtype):
                return B(name, shape, dtype, stB2)
            wff2 = B2("wff2", [128, 32, 1024], BF16)
            hidT = B2("hidT", [128, 32, GB], BF16)
            reluT = Rot([B2(f"reluT{i}", [128, GB], F32) for i in range(2)])
            load_wb(wff2, "wff2", 32, piece=512)

            def ffn1(n):
                for f in range(32):
                    bk = f % 2
                    for kc in range(8):
                        S.pe(lambda e, f=f, kc=kc, bk=bk: e.matmul(banks[bk][:, 0:n], lhsT=wff1.t[:, kc, f * 128:(f + 1) * 128], rhs=hTB.t[:, kc, 0:n],
                                                                   start=(kc == 0), stop=(kc == 7)), reads=[wff1.r, hTB.r], writes=[rb[bk]])
                    rt = reluT.next()
                    S.act(lambda e, bk=bk, rt=rt: e.activation(out=rt.t[:, 0:n], in_=banks[bk][:, 0:n], func=AF.Relu), reads=[rb[bk]], writes=[rt.r])
                    S.dve(lambda e, f=f, rt=rt: e.tensor_tensor(out=hidT.t[:, f, 0:n], in0=rt.t[:, 0:n], in1=rt.t[:, 0:n], op=ALU.mult),
                          reads=[rt.r], writes=[hidT.r])

            def ffn2_gen(tiles, dst_fn):
                for ti, tl in enumerate(tiles):
                    rows = tl.rows
                    xr = xoB.next()
                    S.dma(lambda e, xr=xr, tl=tl, rows=rows: e.dma_start(out=xr.t[0:rows, :], in_=X2[tl.r0:tl.r0 + rows, :]), reads=[r_x2[tl.idx]], writes=[xr.r])
                    for half in range(2):
                        bk = 2 + (ti % 2) * 2 + half
                        for f in range(32):
                            S.pe(lambda e, f=f, half=half, bk=bk, tl=tl, rows=rows: e.matmul(banks[bk][0:rows, :], lhsT=hidT.t[:, f, tl.coff:tl.coff + rows],
                                                                                            rhs=wff2.t[:, f, half * 512:(half + 1) * 512], start=(f == 0), stop=(f == 31)),
                                 reads=[hidT.r, wff2.r], writes=[rb[bk]])
                        S.dve(lambda e, half=half, bk=bk, xr=xr, rows=rows: e.tensor_tensor(out=xr.t[0:rows, half * 512:(half + 1) * 512],
                                                                                            in0=xr.t[0:rows, half * 512:(half + 1) * 512], in1=banks[bk][0:rows, :], op=ALU.add),
                              reads=[xr.r, rb[bk]], writes=[xr.r])
                    S.dma(lambda e, xr=xr, tl=tl, rows=rows: e.dma_start(out=dst_fn(tl), in_=xr.t[0:rows, :]), reads=[xr.r])
                    yield

            def x2norm(tiles):
                return norm_gen(tiles, lambda tl: X2[tl.r0:tl.r0 + tl.rows, :], lambda tl: [r_x2[tl.idx]], 2, hTB, xinB, hnB, ssB, junkB, 6)

            fgroups = [([TT(NT, 64, SEQ, 0)], 64, lambda tl: ys[0:64, :])]
            for g in range(SEQ // GB):
                fgroups.append(([TT(g * tpgB + i, 128, (g * tpgB + i) * 128, i * 128) for i in range(tpgB)], GB, lambda tl: yp[tl.r0:tl.r0 + 128, :]))
            run(x2norm(fgroups[0][0]))
            for gi, (tiles, n, dst_fn) in enumerate(fgroups):
                ffn1(n)
                ng = x2norm(fgroups[gi + 1][0]) if gi + 1 < len(fgroups) else iter(())
                step(ng, 1)
                f2 = ffn2_gen(tiles, dst_fn)
                while step(f2):
                    step(ng, 2)
                run(ng)

    S.finalize()
    print("ops per engine:", {k: len(v) for k, v in S.ops.items()}, flush=True)
    return nc


_NC_CACHE = {}


def kernel(x_prompt, x_sample, cache_k, cache_v, cache_conv, cache_mem_k, cache_mem_v, mem_prompt, rel_table,
           g_mix, w_in, g_q, g_k, lam_vec, g_sub, w_conv, b_conv, ln_g, ln_b, w_out, g_cross, g_mem,
           w_mq, w_mk, w_mv, g_mq, g_mk, w_mo, g_ffn, w_ff1, w_ff2):
    f = lambda a: np.ascontiguousarray(np.asarray(a, dtype=np.float32))
    x_prompt, x_sample = f(x_prompt), f(x_sample)
    cache_k, cache_v, cache_conv = f(cache_k), f(cache_v), f(cache_conv)
    cache_mem_k, cache_mem_v, mem_prompt = f(cache_mem_k), f(cache_mem_v), f(mem_prompt)
    rel = np.arange(-255, 128, dtype=np.int32)
    bkt = rel_bucket_np(rel)
    ohE = np.zeros((32, 383), np.float32)
    ohE[bkt, np.arange(383)] = 1.0
    shared = {
        "rel_table": f(rel_table), "ohE": ohE,
        "g_mix": f(g_mix).reshape(1, D), "w_in": f(w_in)[0], "g_q": f(g_q).reshape(1, 64), "g_k": f(g_k).reshape(1, 64),
        "lam_vec": f(lam_vec).reshape(1, 256), "g_sub": f(g_sub).reshape(1, 128), "w_conv": f(w_conv)[0],
        "b_conv": f(b_conv).reshape(1, 512), "ln_g": f(ln_g).reshape(1, 512), "ln_b": f(ln_b).reshape(1, 512),
        "w_out": f(w_out)[0], "g_cross": f(g_cross).reshape(1, D), "g_mem": f(g_mem).reshape(1, D),
        "w_mq": f(w_mq)[0], "w_mk": f(w_mk)[0], "w_mv": f(w_mv)[0], "g_mq": f(g_mq).reshape(1, 128), "g_mk": f(g_mk).reshape(1, 128),
        "w_mo": f(w_mo)[0], "g_ffn": f(g_ffn).reshape(1, D), "w_ff1": f(w_ff1)[0], "w_ff2": f(w_ff2)[0],
    }
    in_maps = []
    for c in range(NCORES):
        m = dict(shared)
        sl = slice(4 * c, 4 * c + 4)
        m["xp"] = x_prompt[c]
        m["xs"] = x_sample[sl].reshape(64, D)
        m["ck"] = cache_k[0, sl].reshape(4, PAST, 512)
        m["cv"] = cache_v[0, sl].reshape(4, PAST, 512)
        m["cc"] = cache_conv[0, sl]
        m["cmk"] = cache_mem_k[0, sl].reshape(4, 256, 512)
        m["cmv"] = cache_mem_v[0, sl].reshape(4, 256, 512)
        m["memp"] = mem_prompt[c]
        in_maps.append(m)
    if "nc" not in _NC_CACHE:
        _NC_CACHE["nc"] = build_program()
    nc = _NC_CACHE["nc"]
    res = run_bass_kernel_spmd(nc, in_maps, core_ids=list(range(NCORES)))
    R = res.results
    st = lambda k: np.stack([np.asarray(R[c][k], dtype=np.float32) for c in range(NCORES)])
    cat = lambda k: np.concatenate([np.asarray(R[c][k], dtype=np.float32) for c in range(NCORES)], axis=0)
    y_prompt = st("yp")
    y_sample = cat("ys").reshape(32, 16, D)
    k_prompt = st("kp").reshape(1, 8, SEQ, 4, 2, 64)
    v_prompt = st("vp").reshape(1, 8, SEQ, 4, 128)
    conv_prompt = st("cpo").reshape(1, 8, 30, 512)
    mem_k_prompt = st("mkp").reshape(1, 8, 256, 4, 128)
    mem_v_prompt = st("mvp").reshape(1, 8, 256, 4, 128)
    k_sample = cat("kso").reshape(1, 32, 16, 4, 2, 64)
    v_sample = cat("vso").reshape(1, 32, 16, 4, 128)
    conv_sample = cat("cso").reshape(1, 32, 30, 512)
    return (y_prompt, y_sample, k_prompt, v_prompt, conv_prompt, mem_k_prompt, mem_v_prompt, k_sample, v_sample, conv_sample)
```

```python
import math
import types
import numpy as np
import concourse.bass as bass
import concourse.mybir as mybir
from concourse.bass_utils import run_bass_kernel_spmd

F32 = mybir.dt.float32
BF16 = mybir.dt.bfloat16
AF = mybir.ActivationFunctionType
ALU = mybir.AluOpType
AX = mybir.AxisListType

D = 1024
SEQ = 4096
NT = 32
PAST = 4096
EPS = 1e-6
LAM_INIT = 0.8 - 0.6 * math.exp(-0.3 * 0)
NCORES = 8
COMPUTE = ("pe", "act", "dve", "pool")
NDMA_SEMS = 12
_DBG_KIND = {}


class Res:
    __slots__ = ("name", "w", "r", "excl")

    def __init__(self, name, excl=False):
        self.name = name
        self.w = None
        self.r = []
        self.excl = excl


class Op:
    __slots__ = ("eng", "fn", "deps", "is_dma", "dsem", "dval", "inc_val", "needs_inc", "waits")

    def __init__(self, eng, fn, is_dma):
        self.eng = eng
        self.fn = fn
        self.deps = []
        self.is_dma = is_dma
        self.dsem = None
        self.dval = None
        self.inc_val = None
        self.needs_inc = False
        self.waits = []


class Sched:
    def __init__(self, nc):
        self.nc = nc
        self.ops = {e: [] for e in ("pe", "act", "dve", "pool", "sp")}
        self.dma_rr = {"sp": 0, "pool": 0}
        self.dma_last = {}
        self.pending_barrier = {}

    def R(self, name, excl=False):
        return Res(name, excl)

    def barrier(self):
        deps = []
        for eng in COMPUTE:
            for op in reversed(self.ops[eng]):
                if not op.is_dma:
                    deps.append(op)
                    break
        deps.extend(self.dma_last.values())
        for eng in self.ops:
            self.pending_barrier[eng] = list(deps)

    @staticmethod
    def _freeze(fn):
        if fn.__closure__ is None:
            return fn
        cells = []
        for c in fn.__closure__:
            try:
                cells.append(types.CellType(c.cell_contents))
            except ValueError:
                cells.append(c)
        return types.FunctionType(fn.__code__, fn.__globals__, fn.__name__, fn.__defaults__, tuple(cells))

    def _add(self, eng, fn, reads, writes, is_dma=False):
        fn = self._freeze(fn)
        op = Op(eng, fn, is_dma)
        deps = []
        ex = [r for r in reads if r.excl]
        if ex:
            writes = list(writes) + [r for r in ex if r not in writes]
            reads = [r for r in reads if not r.excl]
        for r in reads:
            if r.w is not None:
                deps.append(r.w)
        for w in writes:
            if w.w is not None:
                deps.append(w.w)
            deps.extend(w.r)
        pb = self.pending_barrier.pop(eng, None)
        if pb:
            deps.extend(pb)
        if is_dma:
            slot = self.dma_rr[eng]
            self.dma_rr[eng] = (slot + 1) % NDMA_SEMS
            prev = self.dma_last.get((eng, slot))
            if prev is not None:
                deps.append(prev)
                op.dval = prev.dval + 16
            else:
                op.dval = 16
            op.dsem = (eng, slot)
            self.dma_last[(eng, slot)] = op
        seen = set()
        for d in deps:
            if d is op or id(d) in seen:
                continue
            seen.add(id(d))
            op.deps.append(d)
        for r in reads:
            r.r.append(op)
        for w in writes:
            w.w = op
            w.r = []
        self.ops[eng].append(op)
        return op

    def pe(self, fn, reads=(), writes=()):
        return self._add("pe", fn, reads, writes)

    def act(self, fn, reads=(), writes=()):
        return self._add("act", fn, reads, writes)

    def dve(self, fn, reads=(), writes=()):
        return self._add("dve", fn, reads, writes)

    def pool(self, fn, reads=(), writes=()):
        return self._add("pool", fn, reads, writes)

    def dma(self, fn, reads=(), writes=(), q="sp"):
        return self._add(q, fn, reads, writes, is_dma=True)

    def finalize(self):
        nc = self.nc
        for eng, lst in self.ops.items():
            for op in lst:
                for d in op.deps:
                    if d.is_dma:
                        continue
                    if d.eng == eng and eng == "pe":
                        continue
                    d.needs_inc = True
        for eng in COMPUTE:
            c = 0
            for op in self.ops[eng]:
                if op.is_dma:
                    continue
                if op.needs_inc:
                    c += 1
                    op.inc_val = c
        for eng, lst in self.ops.items():
            waited = {}
            for op in lst:
                need = {}
                for d in op.deps:
                    if d.is_dma:
                        key = ("dma",) + d.dsem
                        val = d.dval
                    else:
                        if d.eng == eng and eng == "pe":
                            continue
                        key = ("eng", d.eng)
                        val = d.inc_val
                    if need.get(key, 0) < val:
                        need[key] = val
                for key, val in need.items():
                    if waited.get(key, 0) >= val:
                        continue
                    waited[key] = val
                    op.waits.append((key, val))
        sems = {}
        for eng in COMPUTE:
            sems[("eng", eng)] = nc.alloc_semaphore(name=f"sem_{eng}")
        for q in ("sp", "pool"):
            for s in range(NDMA_SEMS):
                sems[("dma", q, s)] = nc.alloc_semaphore(name=f"sem_dma_{q}_{s}")
        final = [(("dma", q, slot), op.dval) for (q, slot), op in self.dma_last.items()]
        ops = self.ops

        def replay(eng_name, e):
            for op in ops[eng_name]:
                for key, val in op.waits:
                    e.wait_ge(sems[key], val)
                ins = op.fn(e)
                if op.is_dma:
                    ins.then_inc(sems[("dma",) + op.dsem], 16)
                elif op.needs_inc:
                    ins.then_inc(sems[("eng", op.eng)], 1)

        with nc.Block() as block:
            @block.tensor
            def _(e):
                replay("pe", e)

            @block.scalar
            def _(e):
                replay("act", e)

            @block.vector
            def _(e):
                replay("dve", e)

            @block.gpsimd
            def _(e):
                replay("pool", e)

            @block.sync
            def _(e):
                replay("sp", e)
                for key, val in final:
                    e.wait_ge(sems[key], val)


class Buf:
    def __init__(self, S, nc, name, shape, dtype, stack=None):
        if stack is not None:
            self.t = stack.enter_context(nc.sbuf_tensor(name, shape, dtype))
        else:
            self.t = nc.alloc_sbuf_tensor(name, shape, dtype)
        self.r = S.R(name)


class Rot:
    def __init__(self, items):
        self.items = items
        self.i = 0

    def next(self):
        x = self.items[self.i % len(self.items)]
        self.i += 1
        return x


def rel_bucket_np(rel):
    half, max_exact = 16, 8
    n = np.abs(rel)
    nf = np.maximum(n, 1).astype(np.float32)
    large = (max_exact + (np.log(nf / np.float32(max_exact)) / np.float32(math.log(128 / max_exact))
                          * np.float32(half - max_exact)).astype(np.int32))
    large = np.minimum(large, half - 1)
    return np.where(rel > 0, half, 0) + np.where(n < max_exact, n, large)


def build_program():
    from contextlib import ExitStack
    nc = bass.Bass("TRN2", target_bir_lowering=False)
    S = Sched(nc)

    def din(name, shape):
        return nc.dram_tensor(name, shape, F32, kind="ExternalInput").ap()

    def dout(name, shape):
        return nc.dram_tensor(name, shape, F32, kind="ExternalOutput").ap()

    xp = din("xp", [SEQ, D]); xs = din("xs", [64, D])
    ck = din("ck", [4, PAST, 512]); cv = din("cv", [4, PAST, 512]); cc = din("cc", [4, 30, 512])
    cmk = din("cmk", [4, 256, 512]); cmv = din("cmv", [4, 256, 512]); memp = din("memp", [256, D])
    rel_table = din("rel_table", [32, 4]); ohE = din("ohE", [32, 383])
    g_mix = din("g_mix", [1, D]); w_in = din("w_in", [D, 2560]); g_q = din("g_q", [1, 64]); g_k = din("g_k", [1, 64])
    lam_vec = din("lam_vec", [1, 256]); g_sub = din("g_sub", [1, 128]); w_conv = din("w_conv", [31, 512])
    b_conv = din("b_conv", [1, 512]); ln_g = din("ln_g", [1, 512]); ln_b = din("ln_b", [1, 512])
    w_out = din("w_out", [D, D]); g_cross = din("g_cross", [1, D]); g_mem = din("g_mem", [1, D])
    w_mq = din("w_mq", [D, 512]); w_mk = din("w_mk", [D, 512]); w_mv = din("w_mv", [D, 512])
    g_mq = din("g_mq", [1, 128]); g_mk = din("g_mk", [1, 128]); w_mo = din("w_mo", [512, D])
    g_ffn = din("g_ffn", [1, D]); w_ff1 = din("w_ff1", [D, 4096]); w_ff2 = din("w_ff2", [4096, D])
    yp = dout("yp", [SEQ, D]); ys = dout("ys", [64, D]); kp = dout("kp", [SEQ, 512]); vp = dout("vp", [SEQ, 512])
    cpo = dout("cpo", [30, 512]); mkp = dout("mkp", [256, 512]); mvp = dout("mvp", [256, 512])
    kso = dout("kso", [64, 512]); vso = dout("vso", [64, 512]); cso = dout("cso", [4, 30, 512])
    X1 = nc.dram_tensor("X1", [SEQ + 64, D], F32, **_DBG_KIND).ap()
    X2 = nc.dram_tensor("X2", [SEQ + 64, D], F32, **_DBG_KIND).ap()
    r_x1 = [S.R(f"x1_{t}") for t in range(NT + 1)]
    WB = {}
    for nm_, src_, shp_ in (("wmk", w_mk, [D, 512]), ("wmv", w_mv, [D, 512]), ("wmq", w_mq, [D, 512]), ("wmo", w_mo, [512, D]),
                            ("wff1", w_ff1, [D, 4096]), ("wff2", w_ff2, [4096, D])):
        WB[nm_] = (nc.dram_tensor(nm_ + "_bf", shp_, BF16).ap(), src_, S.R(nm_ + "_bf"), shp_)

    def precast_gen():
        for nm_, nchunk in (("wmk", 1), ("wmv", 1), ("wmq", 1), ("wmo", 1), ("wff1", 4), ("wff2", 4)):
            dst_, src_, res_, shp_ = WB[nm_]
            rows = shp_[0] // nchunk
            for i in range(nchunk):
                S.dma(lambda e, dst_=dst_, src_=src_, i=i, rows=rows: e.dma_start(out=dst_[i * rows:(i + 1) * rows, :], in_=src_[i * rows:(i + 1) * rows, :]),
                      writes=[res_], q="pool")
                yield

    def load_wb(dst, nm_, kchunks, piece=None):
        wsrc, _, res_, shp_ = WB[nm_]
        ncols = shp_[1]
        piece = piece or ncols
        wv = wsrc.rearrange("(k p) c -> p k c", p=128)
        for cs in range(0, ncols, piece):
            S.dma(lambda e, cs=cs: e.dma_start(out=dst.t[:, :, cs:cs + piece], in_=wv[:, :, cs:cs + piece]), reads=[res_], writes=[dst.r])
    r_x2 = [S.R(f"x2_{t}") for t in range(NT + 1)]

    banks = [nc.alloc_psum_tensor(f"bank{i}", [128, 512], F32) for i in range(8)]
    rb = [S.R(f"bank{i}", excl=True) for i in range(8)]

    def B(name, shape, dtype, stack=None):
        return Buf(S, nc, name, shape, dtype, stack)

    identf = B("identf", [128, 128], F32)
    identb = B("identb", [128, 128], BF16)
    onesb = B("onesb", [128, 128], BF16)
    mhalf = B("mhalf", [128, 8], F32)
    gcols = B("gcols", [128, 4, 8], F32)
    gq8 = B("gq8", [128, 64], F32)
    gkb = B("gkb", [128, 64], F32)
    gsubb = B("gsubb", [128, 128], F32)
    gmqb = B("gmqb", [128, 128], F32)
    gmkb = B("gmkb", [128, 128], F32)
    lamv = B("lamv", [128, 256], F32)
    lamt = B("lamt", [128, 8], F32)
    wconv = B("wconv", [128, 4, 31], F32)
    cvec = B("cvec", [128, 3, 4], F32)
    biasT = B("biasT", [128, 2, 4, 128], F32)
    biasHL = B("biasHL", [128, 2, 2, 4, 128], BF16)
    Esb = B("Esb", [32, 383], F32)
    tabsb = B("tabsb", [32, 4], F32)
    cfar = B("cfar", [128, 4], F32)
    stg = B("stg", [32, 128], F32)

    def setup():
        S.pool(lambda e: e.memset(identf.t[:], 0.0), writes=[identf.r])
        S.pool(lambda e: e.affine_select(out=identf.t[:], in_=identf.t[:], pattern=[[-1, 128]], compare_op=ALU.not_equal,
                                         fill=1.0, base=0, channel_multiplier=1), reads=[identf.r], writes=[identf.r])
        S.pool(lambda e: e.tensor_copy(out=identb.t[:], in_=identf.t[:]), reads=[identf.r], writes=[identb.r])
        S.pool(lambda e: e.memset(onesb.t[:], 1.0), writes=[onesb.r])
        S.pool(lambda e: e.memset(mhalf.t[:], -0.5), writes=[mhalf.r])
        def loadT(dst_ap, src_ap, n, dst_res):
            S.dma(lambda e: e.dma_start(out=stg.t[0:n, :], in_=src_ap), writes=[stg.r])
            S.pe(lambda e: e.transpose(out=banks[7][:, 0:n], in_=stg.t[0:n, :], identity=identf.t[0:n, 0:n]), reads=[stg.r, identf.r], writes=[rb[7]])
            S.act(lambda e: e.activation(out=dst_ap, in_=banks[7][:, 0:n], func=AF.Copy), reads=[rb[7]], writes=[dst_res])
        for i, g in enumerate((g_mix, g_cross, g_ffn, g_mem)):
            loadT(gcols.t[:, i, :], g[0, :].rearrange("(k p) -> k p", p=128), 8, gcols.r)
        S.dma(lambda e: e.dma_start(out=gq8.t[:], in_=g_q[0:1, :].partition_broadcast(128)), writes=[gq8.r])
        S.dma(lambda e: e.dma_start(out=gkb.t[:], in_=g_k[0:1, :].partition_broadcast(128)), writes=[gkb.r])
        S.dma(lambda e: e.dma_start(out=gsubb.t[:], in_=g_sub[0:1, :].partition_broadcast(128)), writes=[gsubb.r])
        S.dma(lambda e: e.dma_start(out=gmqb.t[:], in_=g_mq[0:1, :].partition_broadcast(128)), writes=[gmqb.r])
        S.dma(lambda e: e.dma_start(out=gmkb.t[:], in_=g_mk[0:1, :].partition_broadcast(128)), writes=[gmkb.r])
        S.dma(lambda e: e.dma_start(out=lamv.t[:], in_=lam_vec[0:1, :].partition_broadcast(128)), writes=[lamv.r])
        for cb in range(4):
            loadT(wconv.t[:, cb, :], w_conv[:, cb * 128:(cb + 1) * 128], 31, wconv.r)
        for i, v in enumerate((b_conv, ln_g, ln_b)):
            loadT(cvec.t[:, i, :], v[0, :].rearrange("(c p) -> c p", p=128), 4, cvec.r)
        S.dma(lambda e: e.dma_start(out=Esb.t[:], in_=ohE[:, :]), writes=[Esb.r])
        S.dma(lambda e: e.dma_start(out=tabsb.t[:], in_=rel_table[:, :]), writes=[tabsb.r])
        S.dma(lambda e: e.dma_start(out=cfar.t[:], in_=rel_table[15:16, :].partition_broadcast(128)), writes=[cfar.r])
        S.dve(lambda e: e.tensor_scalar(out=gq8.t[:], in0=gq8.t[:], scalar1=0.125, scalar2=None, op0=ALU.mult),
              reads=[gq8.r], writes=[gq8.r])
        S.dve(lambda e: e.tensor_scalar(out=gsubb.t[:], in0=gsubb.t[:], scalar1=1.0 - LAM_INIT, scalar2=None, op0=ALU.mult),
              reads=[gsubb.r], writes=[gsubb.r])
        S.dve(lambda e: e.tensor_scalar(out=gmqb.t[:], in0=gmqb.t[:], scalar1=128.0 ** -0.5, scalar2=None, op0=ALU.mult),
              reads=[gmqb.r], writes=[gmqb.r])
        S.dve(lambda e: e.tensor_scalar(out=cvec.t[:, 1:3, :], in0=cvec.t[:, 1:3, :], scalar1=0.5, scalar2=None, op0=ALU.mult),
              reads=[cvec.r], writes=[cvec.r])
        lt = lamt.t
        S.dve(lambda e: e.tensor_tensor(out=lamv.t[:, 0:64], in0=lamv.t[:, 0:64], in1=lamv.t[:, 64:128], op=ALU.mult),
              reads=[lamv.r], writes=[lamv.r])
        S.dve(lambda e: e.tensor_tensor(out=lamv.t[:, 128:192], in0=lamv.t[:, 128:192], in1=lamv.t[:, 192:256], op=ALU.mult),
              reads=[lamv.r], writes=[lamv.r])
        S.dve(lambda e: e.reduce_sum(out=lt[:, 0:1], in_=lamv.t[:, 0:64], axis=AX.X), reads=[lamv.r], writes=[lamt.r])
        S.dve(lambda e: e.reduce_sum(out=lt[:, 1:2], in_=lamv.t[:, 128:192], axis=AX.X), reads=[lamv.r], writes=[lamt.r])
        S.act(lambda e: e.activation(out=lt[:, 2:4], in_=lt[:, 0:2], func=AF.Exp), reads=[lamt.r], writes=[lamt.r])
        S.dve(lambda e: e.tensor_tensor(out=lt[:, 4:5], in0=lt[:, 2:3], in1=lt[:, 3:4], op=ALU.subtract),
              reads=[lamt.r], writes=[lamt.r])
        S.dve(lambda e: e.tensor_scalar(out=lt[:, 5:6], in0=lt[:, 4:5], scalar1=LAM_INIT, scalar2=-1.0, op0=ALU.add, op1=ALU.mult),
              reads=[lamt.r], writes=[lamt.r])
        for delta in range(2):
            off = 255 - 128 * delta
            bk = banks[delta]
            for q in range(128):
                S.pe(lambda e, q=q, off=off, bk=bk: e.matmul(bk[:, q * 4:(q + 1) * 4], lhsT=Esb.t[:, off - q:off - q + 128],
                                                            rhs=tabsb.t[:, :], start=True, stop=True, skip_group_check=True),
                     reads=[Esb.r, tabsb.r], writes=[rb[delta]])
            S.dve(lambda e, delta=delta, bk=bk: e.tensor_tensor(
                out=biasT.t[:, delta, :, :].rearrange("p h q -> p q h"),
                in0=bk[:, :].rearrange("p (q h) -> p q h", h=4),
                in1=cfar.t[:, :].unsqueeze(1).to_broadcast([128, 128, 4]), op=ALU.subtract),
                reads=[rb[delta], cfar.r], writes=[biasT.r])
        S.dve(lambda e: e.memset(biasT.t[64:128, 0, :, 0:64], -30000.0), reads=[biasT.r], writes=[biasT.r])
        S.dve(lambda e: e.tensor_copy(out=biasHL.t[:, 0], in_=biasT.t[:]), reads=[biasT.r], writes=[biasHL.r])
        S.dve(lambda e: e.tensor_tensor(out=biasHL.t[:, 1], in0=biasT.t[:], in1=biasHL.t[:, 0], op=ALU.subtract),
              reads=[biasT.r, biasHL.r], writes=[biasHL.r])

    def load_w_bf16(dst, dram_ap, kchunks, cols, c0=0, ncols=None, piece=None):
        ncols = cols if ncols is None else ncols
        piece = piece or ncols
        wv = dram_ap.rearrange("(k p) c -> p k c", p=128)
        for cs in range(0, ncols, piece):
            S.dma(lambda e, cs=cs: e.dma_start(out=dst.t[:, :, cs:cs + piece], in_=wv[:, :, c0 + cs:c0 + cs + piece]),
                  writes=[dst.r], q="pool")

    class TT:
        def __init__(self, idx, rows, r0, coff):
            self.idx, self.rows, self.r0, self.coff = idx, rows, r0, coff

    def norm_gen(tiles, src_fn, src_res_fn, gi, hT, xin_rot, hn_rot, ss_rot, junk, pbank):
        for tl in tiles:
            rows = tl.rows
            xb = xin_rot.next(); hn = hn_rot.next(); ss = ss_rot.next()
            S.dma(lambda e, xb=xb, tl=tl: e.dma_start(out=xb.t[0:tl.rows, :], in_=src_fn(tl)), reads=src_res_fn(tl), writes=[xb.r])
            S.act(lambda e, xb=xb, ss=ss, rows=rows: e.activation(out=junk.t[0:rows, :], in_=xb.t[0:rows, :], func=AF.Square,
                                                                  accum_out=ss.t[0:rows, 0:1]),
                  reads=[xb.r], writes=[junk.r, ss.r])
            S.dve(lambda e, ss=ss, rows=rows: e.tensor_scalar(out=ss.t[0:rows, 0:1], in0=ss.t[0:rows, 0:1], scalar1=1.0 / D, scalar2=EPS,
                                                              op0=ALU.mult, op1=ALU.add), reads=[ss.r], writes=[ss.r])
            S.pool(lambda e, ss=ss, rows=rows: e.tensor_tensor(out=ss.t[0:rows, 1:2], in0=ss.t[0:rows, 0:1], in1=mhalf.t[0:rows, 0:1], op=ALU.pow),
                   reads=[ss.r, mhalf.r], writes=[ss.r])
            S.dve(lambda e, xb=xb, hn=hn, ss=ss, rows=rows: e.tensor_scalar(out=hn.t[0:rows, :], in0=xb.t[0:rows, :], scalar1=ss.t[0:rows, 1:2],
                                                                            scalar2=None, op0=ALU.mult), reads=[xb.r, ss.r], writes=[hn.r])
            yield
            pb = banks[pbank][:].bitcast(BF16)
            for kc in range(8):
                S.pe(lambda e, kc=kc, hn=hn, rows=rows: e.transpose(out=pb[:, kc * 128:kc * 128 + rows], in_=hn.t[0:rows, kc * 128:(kc + 1) * 128],
                                                                   identity=identb.t[0:rows, 0:rows]),
                     reads=[hn.r, identb.r], writes=[rb[pbank]])
            S.dve(lambda e, tl=tl, rows=rows, pb=pb: e.tensor_tensor(out=hT.t[:, :, tl.coff:tl.coff + rows],
                                                                     in0=pb[:, :].rearrange("p (k c) -> p k c", c=128)[:, :, 0:rows],
                                                                     in1=gcols.t[:, gi, :].unsqueeze(2).to_broadcast([128, 8, rows]), op=ALU.mult),
                  reads=[rb[pbank], gcols.r], writes=[hT.r])
            yield

    def run(gen):
        for _ in gen:
            pass

    def step(gen, n=1):
        for _ in range(n):
            try:
                next(gen)
            except StopIteration:
                return False
        return True

    def stage_norm(*a):
        run(norm_gen(*a))

    def rstd_groups(ps_ap, ps_res, rows, ngroups, gsize, sq, ssb):
        S.act(lambda e: e.activation(out=sq.t[0:rows, :], in_=ps_ap, func=AF.Square), reads=[ps_res], writes=[sq.r])
        S.dve(lambda e: e.reduce_sum(out=ssb.t[0:rows, 0:ngroups], in_=sq.t[0:rows, :].rearrange("p (g d) -> p g d", d=gsize), axis=AX.X),
              reads=[sq.r], writes=[ssb.r])
        S.dve(lambda e: e.tensor_scalar(out=ssb.t[0:rows, 0:ngroups], in0=ssb.t[0:rows, 0:ngroups], scalar1=1.0 / gsize, scalar2=EPS,
                                        op0=ALU.mult, op1=ALU.add), reads=[ssb.r], writes=[ssb.r])
        S.pool(lambda e: e.tensor_tensor(out=ssb.t[0:rows, ngroups:2 * ngroups], in0=ssb.t[0:rows, 0:ngroups], in1=mhalf.t[0:rows, 0:ngroups],
                                         op=ALU.pow), reads=[ssb.r, mhalf.r], writes=[ssb.r])

    setup()
    with ExitStack() as stA:
        def BA(name, shape, dtype):
            return B(name, shape, dtype, stA)
        GA = 256
        KT = BA("KT", [128, NT, 4, 128], BF16)
        VA = BA("VA", [128, NT, 4, 132], BF16)
        r_kt = [S.R(f"kt{t}") for t in range(NT)]
        r_va = [S.R(f"va{t}") for t in range(NT)]
        win = BA("win", [128, 8, 2560], BF16)
        wout = BA("wout", [128, 8, 1024], BF16)
        xin_rot = Rot([BA(f"xinA{i}", [128, D], F32) for i in range(2)])
        hn_rot = Rot([BA("hnA0", [128, D], BF16)])
        ss_rot = Rot([BA(f"ssA{i}", [128, 2], F32) for i in range(2)])
        junk = hn_rot.items[0]
        hT = BA("hTA", [128, 8, GA], BF16)
        sq = BA("sqA", [128, 512], F32)
        ssq = BA("ssq", [128, 16], F32); ssk = BA("ssk", [128, 16], F32)
        t1 = sq
        knf_rot = Rot([BA(f"knf{i}", [128, 512], F32) for i in range(1)])
        vf_rot = Rot([BA(f"vf{i}", [128, 512], F32) for i in range(1)])
        qnb = BA("qnb", [128, 512], BF16); knb = BA("knb", [128, 512], BF16)
        QT = BA("QT", [128, 4, 64], BF16)
        QB = [BA(f"QB{i}", [128, 4, 2, GA], BF16) for i in range(2)]
        chist2 = BA("chist2", [128, 4, 30 + GA], F32)
        chist = BA("chist", [128, 4, 30 + GA], F32)
        th_rot = Rot([BA(f"thA{i}", [128, GA], F32) for i in range(2)])
        thg_rot = Rot([BA("thG0", [128, GA], F32)])
        acc = BA("accA", [128, 4, GA], F32)
        acc_r = [S.R(f"acc{cb}") for cb in range(4)]
        accb = BA("accb", [128, 4, GA], BF16); sqb = BA("sqb", [128, 4, GA], BF16)
        mean = BA("meanA", [128, GA], F32); rstd = BA("rstdA", [128, GA], F32)
        y_rot = Rot([BA(f"yA{i}", [128, GA], F32) for i in range(2)])
        mixT = BA("mixT", [128, 8, GA], BF16)
        mixT_r = [S.R(f"mixT{c}") for c in range(8)]
        PT_rot = Rot([BA(f"PT{i}", [128, 2, GA], BF16) for i in range(3)])
        ep_rl = Rot([BA(f"eprl{i}", [128, 8], F32) for i in range(2)])
        o0_rot = Rot([BA(f"o0_{i}", [128, 128], F32) for i in range(2)])
        o_rot = Rot([BA(f"o_{i}", [128, 128], F32) for i in range(2)])
        otok_rot = Rot([BA(f"otok{i}", [128, 4, 128], BF16) for i in range(2)])
        ojunk = BA("ojunk", [128, 128], BF16)
        ctmp = BA("ctmp", [30, 512], F32)
        QBD = BA("QBD", [128, 4, 4, 32], BF16)
        KnT = BA("KnT", [128, 4, 64], BF16)
        class View:
            def __init__(self, ap, name):
                self.t = ap
                self.r = S.R(name)
        vaflat = VA.t[:].rearrange("p t h e -> p (t h e)")
        _o = [0]

        def carve(n, pat, name, **kw):
            ap = vaflat[:, _o[0]:_o[0] + n].rearrange(pat, **kw)
            _o[0] += n
            return View(ap, name)
        ktf32 = KT.t[:].rearrange("p t h k -> p (t h k)").bitcast(F32)
        kraw_rot = Rot([View(ktf32[:, i * 2048:(i + 1) * 2048].rearrange("p (t c) -> p t c", c=512), f"kraw{i}") for i in range(2)])
        vraw_rot = Rot([View(ktf32[:, 4096 + i * 2048:4096 + (i + 1) * 2048].rearrange("p (t c) -> p t c", c=512), f"vraw{i}") for i in range(2)])
        vs_rot = Rot([carve(2112, "p (t h e) -> p t h e", f"vs{i}", h=4, e=132) for i in range(2)])
        ktc_rot = Rot([carve(2048, "p (t h k) -> p t h k", f"ktc{i}", h=4, k=128) for i in range(2)])
        pts_rot = Rot([carve(512, "p (t c) -> p t c", f"pts{i}", c=128) for i in range(2)])
        Vn = View(vaflat[0:16, _o[0]:_o[0] + 2112].rearrange("p (s h e) -> p s h e", s=4, h=4), "Vn")
        nears = BA("nears", [128, 128], F32)

        load_w_bf16(win, w_in, 8, 2560, piece=512)
        load_w_bf16(wout, w_out, 8, 1024, piece=512)
        S.dve(lambda e: e.tensor_scalar(out=win.t[:, :, 1536:2048], in0=win.t[:, :, 1536:2048], scalar1=0.5, scalar2=None, op0=ALU.mult),
              reads=[win.r], writes=[win.r])
        for vsb in vs_rot.items:
            S.pool(lambda e, vsb=vsb: e.memset(vsb.t[:, :, :, 128:129], 1.0), writes=[vsb.r])
        S.pool(lambda e: e.memset(Vn.t[:, :, :, 128:129], 1.0), writes=[Vn.r])
        S.pool(lambda e: e.memset(QBD.t[:], 0.0), writes=[QBD.r])
        for qb_ in QB:
            S.pool(lambda e, qb_=qb_: e.memset(qb_.t[:], 0.0), writes=[qb_.r])

        def qkv_gen(tl, sample, qb=None):
            rows = tl.rows

            def proj(cbk, bk):
                for kc in range(8):
                    S.pe(lambda e, kc=kc: e.matmul(banks[bk][0:rows, :], lhsT=hT.t[:, kc, tl.coff:tl.coff + rows],
                                                   rhs=win.t[:, kc, cbk * 512:(cbk + 1) * 512], start=(kc == 0), stop=(kc == 7)),
                         reads=[hT.r, win.r], writes=[rb[bk]])
            proj(0, 6)
            proj(1, 7)
            rstd_groups(banks[6][0:rows, :], rb[6], rows, 8, 64, sq, ssq)
            S.dve(lambda e: e.tensor_tensor(out=t1.t[0:rows, :].rearrange("p (g d) -> p g d", d=64),
                                            in0=banks[6][0:rows, :].rearrange("p (g d) -> p g d", d=64),
                                            in1=ssq.t[0:rows, 8:16].unsqueeze(2).to_broadcast([rows, 8, 64]), op=ALU.mult),
                  reads=[rb[6], ssq.r], writes=[t1.r])
            S.dve(lambda e: e.tensor_tensor(out=qnb.t[0:rows, :].rearrange("p (g d) -> p g d", d=64),
                                            in0=t1.t[0:rows, :].rearrange("p (g d) -> p g d", d=64),
                                            in1=gq8.t[0:rows, :].unsqueeze(1).to_broadcast([rows, 8, 64]), op=ALU.mult),
                  reads=[t1.r, gq8.r], writes=[qnb.r])
            knf = knf_rot.next()
            rstd_groups(banks[7][0:rows, :], rb[7], rows, 8, 64, sq, ssk)
            S.dve(lambda e: e.tensor_tensor(out=t1.t[0:rows, :].rearrange("p (g d) -> p g d", d=64),
                                            in0=banks[7][0:rows, :].rearrange("p (g d) -> p g d", d=64),
                                            in1=ssk.t[0:rows, 8:16].unsqueeze(2).to_broadcast([rows, 8, 64]), op=ALU.mult),
                  reads=[rb[7], ssk.r], writes=[t1.r])
            S.dve(lambda e: e.tensor_tensor(out=knf.t[0:rows, :].rearrange("p (g d) -> p g d", d=64),
                                            in0=t1.t[0:rows, :].rearrange("p (g d) -> p g d", d=64),
                                            in1=gkb.t[0:rows, :].unsqueeze(1).to_broadcast([rows, 8, 64]), op=ALU.mult),
                  reads=[t1.r, gkb.r], writes=[knf.r])
            kdst = kso[0:64, :] if sample else kp[tl.r0:tl.r0 + rows, :]
            S.dma(lambda e: e.dma_start(out=kdst, in_=knf.t[0:rows, :]), reads=[knf.r])
            S.pool(lambda e: e.tensor_copy(out=knb.t[0:rows, :], in_=knf.t[0:rows, :]), reads=[knf.r], writes=[knb.r])
            yield
            proj(2, 6)
            vf = vf_rot.next()
            S.act(lambda e: e.activation(out=vf.t[0:rows, :], in_=banks[6][0:rows, :], func=AF.Copy), reads=[rb[6]], writes=[vf.r])
            vdst = vso[0:64, :] if sample else vp[tl.r0:tl.r0 + rows, :]
            S.dma(lambda e: e.dma_start(out=vdst, in_=vf.t[0:rows, :]), reads=[vf.r])
            if not sample:
                S.pool(lambda e: e.tensor_copy(out=VA.t[:, tl.idx, :, 0:128], in_=vf.t[:, :].rearrange("p (h e) -> p h e", e=128)),
                       reads=[vf.r], writes=[r_va[tl.idx]])
            pb = banks[7][:].bitcast(BF16)
            for h in range(4):
                S.pe(lambda e, h=h: e.transpose(out=pb[:, h * 128:h * 128 + rows], in_=qnb.t[0:rows, h * 128:(h + 1) * 128],
                                                identity=identb.t[0:rows, 0:rows]), reads=[qnb.r, identb.r], writes=[rb[7]])
            for h in range(4):
                S.pe(lambda e, h=h: e.transpose(out=pb[:, 512 + h * 128:512 + h * 128 + rows], in_=knb.t[0:rows, h * 128:(h + 1) * 128],
                                                identity=identb.t[0:rows, 0:rows]), reads=[knb.r, identb.r], writes=[rb[7]])
            qv = pb[:, 0:512].rearrange("p (h k) -> p h k", k=128)
            if sample:
                S.act(lambda e: e.activation(out=QT.t[:, :, tl.coff:tl.coff + rows], in_=qv[:, :, 0:rows], func=AF.Copy),
                      reads=[rb[7]], writes=[QT.r])
                S.dve(lambda e: e.tensor_copy(out=KnT.t[:, :, 0:rows],
                                              in_=pb[:, 512:1024].rearrange("p (h k) -> p h k", k=128)[:, :, 0:rows]),
                      reads=[rb[7]], writes=[KnT.r])
            else:
                S.act(lambda e: e.activation(out=qb.t[0:64, :, 0, tl.coff:tl.coff + rows], in_=qv[0:64, :, 0:rows], func=AF.Copy),
                      reads=[rb[7]], writes=[qb.r])
                S.act(lambda e: e.activation(out=qb.t[64:128, :, 1, tl.coff:tl.coff + rows], in_=qv[64:128, :, 0:rows], func=AF.Copy),
                      reads=[rb[7]], writes=[qb.r])
                S.dve(lambda e: e.tensor_copy(out=KT.t[:, tl.idx, :, :], in_=pb[:, 512:1024].rearrange("p (h k) -> p h k", k=128)),
                      reads=[rb[7]], writes=[r_kt[tl.idx]])
            yield

        def glu_gen(n, nseq, L, ch):
            hv = ch.t[:, :, 0:nseq * (30 + L)].rearrange("p c (s l) -> p c s l", s=nseq)
            for cb in range(4):
                pa, pg = 6, 7
                for which, bk in ((0, pa), (1, pg)):
                    c0 = 1536 + which * 512 + cb * 128
                    for kc in range(8):
                        S.pe(lambda e, kc=kc, c0=c0, bk=bk: e.matmul(banks[bk][:, 0:n], lhsT=win.t[:, kc, c0:c0 + 128], rhs=hT.t[:, kc, 0:n],
                                                                     start=(kc == 0), stop=(kc == 7)),
                             reads=[win.r, hT.r], writes=[rb[bk]])
                th = thg_rot.next()
                S.act(lambda e, th=th, pg=pg: e.activation(out=th.t[:, 0:n], in_=banks[pg][:, 0:n], func=AF.Tanh, scale=0.5),
                      reads=[rb[pg]], writes=[th.r])
                S.dve(lambda e, th=th, pa=pa, cb=cb: e.scalar_tensor_tensor(
                    out=hv[:, cb, :, 30:30 + L], in0=th.t[:, 0:n].rearrange("p (s l) -> p s l", s=nseq), scalar=1.0,
                    in1=banks[pa][:, 0:n].rearrange("p (s l) -> p s l", s=nseq), op0=ALU.add, op1=ALU.mult),
                    reads=[th.r, rb[pa]], writes=[ch.r])
                yield

        def stage_qkv(tl, sample, qb=None):
            run(qkv_gen(tl, sample, qb))

        def stage_glu(n, nseq, L, ch=None):
            run(glu_gen(n, nseq, L, ch or chist))

        def conv_ln_gen(n, nseq, L, ch=None):
            ch = ch or chist
            hv = ch.t[:, :, 0:nseq * (30 + L)].rearrange("p c (s l) -> p c s l", s=nseq)
            avs = [acc.t[:, cb, 0:n].rearrange("p (s l) -> p s l", s=nseq) for cb in range(4)]
            for cb in range(4):
                S.dve(lambda e, cb=cb: e.tensor_scalar(out=avs[cb], in0=hv[:, cb, :, 0:L], scalar1=wconv.t[:, cb, 0:1],
                                                       scalar2=cvec.t[:, 0, cb:cb + 1], op0=ALU.mult, op1=ALU.add),
                      reads=[ch.r, wconv.r, cvec.r], writes=[acc_r[cb]])
            yield
            for j in range(1, 31):
                for cb in range(4):
                    S.dve(lambda e, cb=cb, j=j: e.scalar_tensor_tensor(out=avs[cb], in0=hv[:, cb, :, j:j + L], scalar=wconv.t[:, cb, j:j + 1],
                                                                       in1=avs[cb], op0=ALU.mult, op1=ALU.add),
                          reads=[ch.r, wconv.r, acc_r[cb]], writes=[acc_r[cb]])
                yield
            S.act(lambda e: e.activation(out=accb.t[:, :, 0:n], in_=acc.t[:, :, 0:n], func=AF.Copy), reads=acc_r, writes=[accb.r])
            S.act(lambda e: e.activation(out=sqb.t[:, :, 0:n], in_=acc.t[:, :, 0:n], func=AF.Square), reads=acc_r, writes=[sqb.r])
            yield
            for cb in range(4):
                S.pe(lambda e, cb=cb: e.matmul(banks[6][:, 0:n], lhsT=onesb.t[:, :], rhs=accb.t[:, cb, 0:n], start=(cb == 0), stop=(cb == 3)),
                     reads=[onesb.r, accb.r], writes=[rb[6]])
            for cb in range(4):
                S.pe(lambda e, cb=cb: e.matmul(banks[7][:, 0:n], lhsT=onesb.t[:, :], rhs=sqb.t[:, cb, 0:n], start=(cb == 0), stop=(cb == 3)),
                     reads=[onesb.r, sqb.r], writes=[rb[7]])
            S.dve(lambda e: e.tensor_scalar(out=mean.t[:, 0:n], in0=banks[6][:, 0:n], scalar1=1.0 / 512, scalar2=None, op0=ALU.mult),
                  reads=[rb[6]], writes=[mean.r])
            S.dve(lambda e: e.tensor_tensor(out=rstd.t[:, 0:n], in0=mean.t[:, 0:n], in1=mean.t[:, 0:n], op=ALU.mult),
                  reads=[mean.r], writes=[rstd.r])
            S.dve(lambda e: e.scalar_tensor_tensor(out=rstd.t[:, 0:n], in0=banks[7][:, 0:n], scalar=1.0 / 512, in1=rstd.t[:, 0:n],
                                                   op0=ALU.mult, op1=ALU.subtract), reads=[rb[7], rstd.r], writes=[rstd.r])
            S.dve(lambda e: e.tensor_scalar(out=rstd.t[:, 0:n], in0=rstd.t[:, 0:n], scalar1=EPS, scalar2=None, op0=ALU.add),
                  reads=[rstd.r], writes=[rstd.r])
            S.act(lambda e: e.activation(out=rstd.t[:, 0:n], in_=rstd.t[:, 0:n], func=AF.Ln), reads=[rstd.r], writes=[rstd.r])
            S.act(lambda e: e.activation(out=rstd.t[:, 0:n], in_=rstd.t[:, 0:n], func=AF.Exp, scale=-0.5), reads=[rstd.r], writes=[rstd.r])
            yield
            for cb in range(4):
                y = y_rot.next(); th = th_rot.next()
                S.pool(lambda e, cb=cb, y=y: e.tensor_tensor(out=y.t[:, 0:n], in0=acc.t[:, cb, 0:n], in1=mean.t[:, 0:n], op=ALU.subtract),
                       reads=[acc_r[cb], mean.r], writes=[y.r])
                S.pool(lambda e, y=y: e.tensor_tensor(out=y.t[:, 0:n], in0=y.t[:, 0:n], in1=rstd.t[:, 0:n], op=ALU.mult),
                       reads=[y.r, rstd.r], writes=[y.r])
                S.dve(lambda e, cb=cb, y=y: e.tensor_scalar(out=y.t[:, 0:n], in0=y.t[:, 0:n], scalar1=cvec.t[:, 1, cb:cb + 1],
                                                            scalar2=cvec.t[:, 2, cb:cb + 1], op0=ALU.mult, op1=ALU.add),
                      reads=[y.r, cvec.r], writes=[y.r])
                S.act(lambda e, y=y, th=th: e.activation(out=th.t[:, 0:n], in_=y.t[:, 0:n], func=AF.Tanh), reads=[y.r], writes=[th.r])
                S.dve(lambda e, cb=cb, y=y, th=th: e.scalar_tensor_tensor(out=mixT.t[:, 4 + cb, 0:n], in0=th.t[:, 0:n], scalar=1.0, in1=y.t[:, 0:n],
                                                                          op0=ALU.add, op1=ALU.mult),
                      reads=[th.r, y.r], writes=[mixT_r[4 + cb]])
                yield

        def stage_conv_ln(n, nseq, L):
            run(conv_ln_gen(n, nseq, L))

        def conv_out(dst_ap, src_view):
            for cb in range(4):
                S.pe(lambda e, cb=cb: e.transpose(out=banks[2][0:30, cb * 128:(cb + 1) * 128], in_=src_view(cb), identity=identf.t[:, :]),
                     reads=[chist.r, identf.r], writes=[rb[2]])
            S.act(lambda e: e.activation(out=ctmp.t[:, :], in_=banks[2][0:30, :], func=AF.Copy), reads=[rb[2]], writes=[ctmp.r])
            S.dma(lambda e: e.dma_start(out=dst_ap, in_=ctmp.t[:, :]), reads=[ctmp.r])

        def epilogue(obank, rows, otok, h, qoff=0):
            rl = ep_rl.next(); o0 = o0_rot.next(); o = o_rot.next()
            ob = banks[obank]
            S.dve(lambda e: e.reciprocal(out=rl.t[0:rows, 0:1], in_=ob[0:rows, 128:129]), reads=[rb[obank]], writes=[rl.r])
            S.dve(lambda e: e.reciprocal(out=rl.t[0:rows, 1:2], in_=ob[0:rows, 260:261]), reads=[rb[obank]], writes=[rl.r])
            S.dve(lambda e: e.tensor_scalar(out=rl.t[0:rows, 2:3], in0=rl.t[0:rows, 1:2], scalar1=lamt.t[0:rows, 5:6], scalar2=None, op0=ALU.mult),
                  reads=[rl.r, lamt.r], writes=[rl.r])
            S.dve(lambda e: e.tensor_scalar(out=o0.t[0:rows, :], in0=ob[0:rows, 0:128], scalar1=rl.t[0:rows, 0:1], scalar2=None, op0=ALU.mult),
                  reads=[rb[obank], rl.r], writes=[o0.r])
            S.dve(lambda e: e.scalar_tensor_tensor(out=o.t[0:rows, :], in0=ob[0:rows, 132:260], scalar=rl.t[0:rows, 2:3], in1=o0.t[0:rows, :],
                                                   op0=ALU.mult, op1=ALU.add), reads=[rb[obank], rl.r, o0.r], writes=[o.r])
            S.pool(lambda e: e.tensor_tensor(out=o0.t[0:rows, :], in0=o.t[0:rows, :], in1=o.t[0:rows, :], op=ALU.mult), reads=[o.r], writes=[o0.r])
            S.dve(lambda e: e.reduce_sum(out=rl.t[0:rows, 3:4], in_=o0.t[0:rows, :], axis=AX.X), reads=[o0.r], writes=[rl.r])
            S.dve(lambda e: e.tensor_scalar(out=rl.t[0:rows, 3:4], in0=rl.t[0:rows, 3:4], scalar1=1.0 / 128, scalar2=EPS, op0=ALU.mult, op1=ALU.add),
                  reads=[rl.r], writes=[rl.r])
            S.pool(lambda e: e.tensor_tensor(out=rl.t[0:rows, 4:5], in0=rl.t[0:rows, 3:4], in1=mhalf.t[0:rows, 0:1], op=ALU.pow),
                   reads=[rl.r, mhalf.r], writes=[rl.r])
            S.dve(lambda e: e.scalar_tensor_tensor(out=otok.t[0:rows, h, :], in0=o.t[0:rows, :], scalar=rl.t[0:rows, 4:5], in1=gsubb.t[0:rows, :],
                                                   op0=ALU.mult, op1=ALU.mult), reads=[o.r, rl.r, gsubb.r], writes=[otok.r])

        def otok_to_mixT(otok, rows, coff):
            pb = banks[0][:].bitcast(BF16)
            for h in range(4):
                S.pe(lambda e, h=h: e.transpose(out=pb[:, h * 128:h * 128 + rows], in_=otok.t[0:rows, h, :], identity=identb.t[0:rows, 0:rows]),
                     reads=[otok.r, identb.r], writes=[rb[0]])
            S.act(lambda e: e.activation(out=mixT.t[:, 0:4, coff:coff + rows],
                                         in_=pb[:, 0:512].rearrange("p (h k) -> p h k", k=128)[:, :, 0:rows], func=AF.Copy),
                  reads=[rb[0]], writes=mixT_r[0:4])

        def stage_wout(tl, src_ap, dst_ap, dst_res):
            rows = tl.rows
            xr = xin_rot.next()
            S.dma(lambda e: e.dma_start(out=xr.t[0:rows, :], in_=src_ap), writes=[xr.r])
            for half in range(2):
                bk = half
                for c in range(8):
                    S.pe(lambda e, c=c, half=half, bk=bk: e.matmul(banks[bk][0:rows, :], lhsT=mixT.t[:, c, tl.coff:tl.coff + rows],
                                                                   rhs=wout.t[:, c, half * 512:(half + 1) * 512], start=(c == 0), stop=(c == 7)),
                         reads=[mixT_r[c], wout.r], writes=[rb[bk]])
                S.dve(lambda e, half=half, bk=bk: e.tensor_tensor(out=xr.t[0:rows, half * 512:(half + 1) * 512], in0=xr.t[0:rows, half * 512:(half + 1) * 512],
                                                                  in1=banks[bk][0:rows, :], op=ALU.add), reads=[xr.r, rb[bk]], writes=[xr.r])
            S.dma(lambda e: e.dma_start(out=dst_ap, in_=xr.t[0:rows, :]), reads=[xr.r], writes=[dst_res])

        stl = TT(NT, 64, SEQ, 0)
        stage_norm([stl], lambda tl: xs[0:64, :], lambda tl: [], 0, hT, xin_rot, hn_rot, ss_rot, junk, 3)
        stage_qkv(stl, True)
        for s in range(4):
            for m in range(2):
                S.pool(lambda e, s=s, m=m: e.tensor_copy(out=QBD.t[m * 64:(m + 1) * 64, s, :, m * 16:(m + 1) * 16],
                                                         in_=QT.t[m * 64:(m + 1) * 64, :, s * 16:(s + 1) * 16]),
                       reads=[QT.r], writes=[QBD.r])
        for s in range(4):
            for kc in range(8):
                S.pe(lambda e, s=s, kc=kc: e.matmul(banks[2][0:16, :], lhsT=hT.t[:, kc, s * 16:(s + 1) * 16], rhs=win.t[:, kc, 1024:1536],
                                                    start=(kc == 0), stop=(kc == 7)), reads=[hT.r, win.r], writes=[rb[2]])
            S.act(lambda e, s=s: e.activation(out=Vn.t[0:16, s, :, 0:128], in_=banks[2][0:16, :].rearrange("p (h e) -> p h e", e=128), func=AF.Copy),
                  reads=[rb[2]], writes=[Vn.r])
        hvs = chist.t[:, :, 0:4 * 46].rearrange("p c (s l) -> p c s l", s=4)
        for s in range(4):
            S.dma(lambda e, s=s: e.dma_start(out=ctmp.t[:, :], in_=cc[s]), writes=[ctmp.r])
            for cb in range(4):
                S.pe(lambda e, s=s, cb=cb: e.transpose(out=banks[2][:, cb * 32:cb * 32 + 30], in_=ctmp.t[0:30, cb * 128:(cb + 1) * 128],
                                                       identity=identf.t[0:30, 0:30]), reads=[ctmp.r, identf.r], writes=[rb[2]])
            S.act(lambda e, s=s: e.activation(out=hvs[:, :, s, 0:30], in_=banks[2][:, 0:128].rearrange("p (c l) -> p c l", l=32)[:, :, 0:30],
                                              func=AF.Copy), reads=[rb[2]], writes=[chist.r])
        stage_glu(64, 4, 16)
        for s in range(4):
            conv_out(cso[s], lambda cb, s=s: hvs[:, cb, s, 16:46])
        stage_conv_ln(64, 4, 16)

        sb_otok = otok_rot.next()
        for s in range(4):
            obanks = (5, 6, 7)

            def oacc(a):
                return banks[5 + a // 3][0:16, (a % 3) * 132:(a % 3) * 132 + 129], 5 + a // 3
            first_in_bank = {5: True, 6: True, 7: True}
            nchunks = 8
            for c in range(nchunks + 1):
                last = (c == nchunks)
                if not last:
                    kraw = kraw_rot.next(); vraw = vraw_rot.next(); vsb = vs_rot.next(); ktc = ktc_rot.next(); pts = pts_rot.next()
                    S.dma(lambda e, kraw=kraw, c=c, s=s: e.dma_start(out=kraw.t[:, :, :], in_=ck[s, c * 512:(c + 1) * 512, :].rearrange("(t p) c -> p t c", p=128)),
                          writes=[kraw.r])
                    S.dma(lambda e, vraw=vraw, c=c, s=s: e.dma_start(out=vraw.t[:, :, :], in_=cv[s, c * 512:(c + 1) * 512, :].rearrange("(t p) c -> p t c", p=128)),
                          writes=[vraw.r])
                    tbk = (0, 1, 3, 4)
                    for tt in range(4):
                        bk = tbk[tt]
                        for h in range(4):
                            S.pe(lambda e, tt=tt, h=h, kraw=kraw, bk=bk: e.transpose(
                                out=banks[bk][:, h * 128:(h + 1) * 128], in_=kraw.t[:, tt, h * 128:(h + 1) * 128],
                                identity=identf.t[:, :]), reads=[kraw.r, identf.r], writes=[rb[bk]])
                        if tt % 2 == 0:
                            S.act(lambda e, tt=tt, ktc=ktc, bk=bk: e.activation(out=ktc.t[:, tt, :, :].rearrange("p h k -> p (h k)"),
                                                                                in_=banks[bk][:, :], func=AF.Copy), reads=[rb[bk]], writes=[ktc.r])
                            S.pool(lambda e, tt=tt, vsb=vsb, vraw=vraw: e.tensor_copy(out=vsb.t[:, tt, :, 0:128],
                                                                                      in_=vraw.t[:, tt, :].rearrange("p (h e) -> p h e", e=128)),
                                   reads=[vraw.r], writes=[vsb.r])
                        else:
                            S.dve(lambda e, tt=tt, ktc=ktc, bk=bk: e.tensor_copy(out=ktc.t[:, tt, :, :].rearrange("p h k -> p (h k)"),
                                                                                 in_=banks[bk][:, :]), reads=[rb[bk]], writes=[ktc.r])
                            S.act(lambda e, tt=tt, vsb=vsb, vraw=vraw: e.activation(out=vsb.t[:, tt, :, 0:128],
                                                                                    in_=vraw.t[:, tt, :].rearrange("p (h e) -> p h e", e=128), func=AF.Copy),
                                  reads=[vraw.r], writes=[vsb.r])
                    ntl, krows = 4, 128
                    for tt in range(4):
                        for h in range(4):
                            S.pe(lambda e, tt=tt, h=h, ktc=ktc, s=s: e.matmul(banks[2][:, tt * 128 + h * 32:tt * 128 + (h + 1) * 32], lhsT=ktc.t[:, tt, h, :],
                                                                             rhs=QBD.t[:, s, h, :], start=True, stop=True, skip_group_check=True),
                                 reads=[ktc.r, QBD.r], writes=[rb[2]])
                    if c == nchunks - 1:
                        S.dve(lambda e: e.tensor_tensor(out=nears.t[:, :].rearrange("p (h m q) -> p h m q", h=4, m=2),
                                                        in0=banks[2][:, 384:512].rearrange("p (h m q) -> p h m q", h=4, m=2),
                                                        in1=biasT.t[:, 1, :, 0:16].unsqueeze(2).to_broadcast([128, 4, 2, 16]), op=ALU.add),
                              reads=[rb[2], biasT.r], writes=[nears.r])
                        S.act(lambda e, pts=pts: e.activation(out=pts.t[:, 0:3, :], in_=banks[2][:, 0:384].rearrange("p (t c) -> p t c", c=128), func=AF.Exp),
                              reads=[rb[2]], writes=[pts.r])
                        S.act(lambda e, pts=pts: e.activation(out=pts.t[:, 3, :], in_=nears.t[:, :], func=AF.Exp), reads=[nears.r], writes=[pts.r])
                    else:
                        S.act(lambda e, pts=pts: e.activation(out=pts.t[:, :, :], in_=banks[2][:, :].rearrange("p (t c) -> p t c", c=128), func=AF.Exp),
                              reads=[rb[2]], writes=[pts.r])
                    vsrc = lambda tt, h, vsb=vsb: vsb.t[:, tt, h, 0:129]
                    vres = vsb.r
                else:
                    pts = pts_rot.next()
                    ntl, krows = 1, 16
                    for h in range(4):
                        S.pe(lambda e, h=h, s=s: e.matmul(banks[2][0:16, h * 32:(h + 1) * 32], lhsT=KnT.t[:, h, s * 16:(s + 1) * 16], rhs=QBD.t[:, s, h, :],
                                                          start=True, stop=True, skip_group_check=True), reads=[KnT.r, QBD.r], writes=[rb[2]])
                    S.dve(lambda e: e.tensor_tensor(out=nears.t[0:16, :].rearrange("p (h m q) -> p h m q", h=4, m=2),
                                                    in0=banks[2][0:16, 0:128].rearrange("p (h m q) -> p h m q", h=4, m=2),
                                                    in1=biasT.t[0:16, 0, :, 0:16].unsqueeze(2).to_broadcast([16, 4, 2, 16]), op=ALU.add),
                          reads=[rb[2], biasT.r], writes=[nears.r])
                    S.act(lambda e, pts=pts: e.activation(out=pts.t[0:16, 0, :], in_=nears.t[0:16, :], func=AF.Exp), reads=[nears.r], writes=[pts.r])
                    vsrc = lambda tt, h, s=s: Vn.t[0:16, s, h, 0:129]
                    vres = Vn.r
                for tt in range(ntl):
                    for a in range(8):
                        oap, obk = oacc(a)
                        st = first_in_bank[obk]
                        first_in_bank[obk] = False
                        S.pe(lambda e, tt=tt, a=a, oap=oap, st=st, pts=pts, vsrc=vsrc, krows=krows, last=last: e.matmul(
                            oap, lhsT=pts.t[0:krows, tt, a * 16:(a + 1) * 16], rhs=vsrc(tt, a // 2), start=st, stop=(last),
                            skip_group_check=True), reads=[pts.r, vres], writes=[rb[obk]])
            for h in range(4):
                rl = ep_rl.next(); o0 = o0_rot.next(); o = o_rot.next()
                a0, a1 = 2 * h, 2 * h + 1
                p0, bk0 = oacc(a0); p1, bk1 = oacc(a1)
                S.dve(lambda e, p0=p0, rl=rl: e.reciprocal(out=rl.t[0:16, 0:1], in_=p0[:, 128:129]), reads=[rb[bk0]], writes=[rl.r])
                S.dve(lambda e, p1=p1, rl=rl: e.reciprocal(out=rl.t[0:16, 1:2], in_=p1[:, 128:129]), reads=[rb[bk1]], writes=[rl.r])
                S.dve(lambda e, rl=rl: e.tensor_scalar(out=rl.t[0:16, 2:3], in0=rl.t[0:16, 1:2], scalar1=lamt.t[0:16, 5:6], scalar2=None, op0=ALU.mult),
                      reads=[rl.r, lamt.r], writes=[rl.r])
                S.act(lambda e, p0=p0, rl=rl, o0=o0: e.activation(out=o0.t[0:16, :], in_=p0[:, 0:128], func=AF.Copy, scale=rl.t[0:16, 0:1]),
                      reads=[rb[bk0], rl.r], writes=[o0.r])
                S.dve(lambda e, p1=p1, rl=rl, o0=o0, o=o: e.scalar_tensor_tensor(out=o.t[0:16, :], in0=p1[:, 0:128], scalar=rl.t[0:16, 2:3], in1=o0.t[0:16, :],
                                                                                 op0=ALU.mult, op1=ALU.add), reads=[rb[bk1], rl.r, o0.r], writes=[o.r])
                S.act(lambda e, rl=rl, o=o: e.activation(out=ojunk.t[0:16, :], in_=o.t[0:16, :], func=AF.Square, accum_out=rl.t[0:16, 3:4]),
                      reads=[o.r], writes=[ojunk.r, rl.r])
                S.dve(lambda e, rl=rl: e.tensor_scalar(out=rl.t[0:16, 3:4], in0=rl.t[0:16, 3:4], scalar1=1.0 / 128, scalar2=EPS, op0=ALU.mult, op1=ALU.add),
                      reads=[rl.r], writes=[rl.r])
                S.pool(lambda e, rl=rl: e.tensor_tensor(out=rl.t[0:16, 4:5], in0=rl.t[0:16, 3:4], in1=mhalf.t[0:16, 0:1], op=ALU.pow),
                       reads=[rl.r, mhalf.r], writes=[rl.r])
                S.dve(lambda e, rl=rl, o=o, h=h: e.scalar_tensor_tensor(out=sb_otok.t[0:16, h, :], in0=o.t[0:16, :], scalar=rl.t[0:16, 4:5],
                                                                        in1=gsubb.t[0:16, :], op0=ALU.mult, op1=ALU.mult),
                      reads=[o.r, rl.r, gsubb.r], writes=[sb_otok.r])
            otok_to_mixT(sb_otok, 16, s * 16)
            sb_otok = otok_rot.next()
        stage_wout(stl, xs[0:64, :], X1[SEQ:SEQ + 64, :], r_x1[NT])

        S.barrier()
        S.pool(lambda e: e.memset(VA.t[:, :, :, 128:129], 1.0), writes=r_va)
        pcg = precast_gen()
        NG = SEQ // GA
        tpg = GA // 128

        def attn_gen(g, otoks, qb):
            i0 = g * tpg
            nkt = i0 + tpg
            for h in range(4):
                ob = [2 + (h % 2) * 2 + i for i in range(tpg)]
                started = [False] * tpg
                sbuf_i = [0]

                def emit_scores(j, h=h):
                    sb = sbuf_i[0] % 2
                    sbuf_i[0] += 1
                    il0 = max(0, j - i0)
                    c0 = il0 * 128
                    sv = banks[sb][:, :].rearrange("p (m q) -> p m q", m=2)
                    S.pe(lambda e, sb=sb, j=j: e.matmul(banks[sb][:, :], lhsT=KT.t[:, j, h, :], rhs=qb.t[:, h, :, :].rearrange("p m q -> p (m q)"),
                                                        start=True, stop=True, skip_group_check=True),
                         reads=[r_kt[j], qb.r], writes=[rb[sb]])
                    for il in range(il0, tpg):
                        dlt = (i0 + il) - j
                        if dlt >= 2:
                            break
                        for m in range(2):
                            for part in range(2):
                                S.pe(lambda e, sv=sv, m=m, part=part, il=il, dlt=dlt: e.matmul(
                                    sv[:, m, il * 128:(il + 1) * 128], lhsT=identb.t[:, :], rhs=biasHL.t[:, part, dlt, h, :],
                                    start=False, stop=True, skip_group_check=True), reads=[identb.r, biasHL.r], writes=[rb[sb]])
                    return sb, il0

                pend = emit_scores(0)
                for j in range(nkt):
                    sb, il0 = pend
                    if j + 1 < nkt:
                        pend = emit_scores(j + 1)
                    pt = PT_rot.next()
                    sv = banks[sb][:, :].rearrange("p (m q) -> p m q", m=2)
                    S.act(lambda e, c0=il0 * 128, sv=sv, pt=pt: e.activation(out=pt.t[:, :, c0:GA], in_=sv[:, :, c0:GA], func=AF.Exp),
                          reads=[rb[sb]], writes=[pt.r])
                    for il in range(il0, tpg):
                        lastk = (j == i0 + il)
                        for m in range(2):
                            st = not started[il]
                            started[il] = True
                            S.pe(lambda e, m=m, il=il, pt=pt, st=st, lastk=lastk, j=j: e.matmul(
                                banks[ob[il]][:, m * 132:m * 132 + 129], lhsT=pt.t[:, m, il * 128:(il + 1) * 128], rhs=VA.t[:, j, h, 0:129],
                                start=st, stop=lastk, skip_group_check=True), reads=[pt.r, r_va[j]], writes=[rb[ob[il]]])
                        if lastk:
                            epilogue(ob[il], 128, otoks[il], h)
                    yield

        def gtiles(g):
            return [TT(g * tpg + i, 128, (g * tpg + i) * 128, i * 128) for i in range(tpg)]

        chs = [chist, chist2]

        def s1_gen(g):
            ch = chs[g % 2]
            yield from norm_gen(gtiles(g), lambda tl: xp[tl.r0:tl.r0 + 128, :], lambda tl: [], 0, hT, xin_rot, hn_rot, ss_rot, junk, 6)
            for tl in gtiles(g):
                yield from qkv_gen(tl, False, QB[g % 2])
            yield from glu_gen(GA, 1, GA, ch)
            if g == 0:
                S.pool(lambda e: e.memset(ch.t[:, :, 0:30], 0.0), reads=[ch.r], writes=[ch.r])
            else:
                pch = chs[(g - 1) % 2]
                S.pool(lambda e: e.tensor_copy(out=ch.t[:, :, 0:30], in_=pch.t[:, :, GA:GA + 30]), reads=[pch.r, ch.r], writes=[ch.r])
            yield

        run(s1_gen(0))
        for g in range(NG):
            tiles = gtiles(g)
            ch = chs[g % 2]
            if g == NG - 1:
                conv_out(cpo[:, :], lambda cb: ch.t[:, cb, GA:GA + 30])
            otoks = [otok_rot.next() for _ in range(tpg)]
            step(pcg)
            ag = attn_gen(g, otoks, QB[g % 2])
            cg = conv_ln_gen(GA, 1, GA, ch)
            ng = s1_gen(g + 1) if g + 1 < NG else iter(())
            nsteps = 4 * (g * tpg + tpg)
            per_c = max(1, -(-38 // nsteps))
            n_s1 = 2 * tpg + 2 * tpg + 4 + 1
            per_n = max(1, -(-n_s1 // nsteps))
            every_n = max(1, nsteps // n_s1)
            every_c = max(1, (3 * nsteps // 4) // 38)
            k = 0
            while step(ag):
                k += 1
                if k % every_c == 0:
                    step(cg, per_c)
                if k % every_n == 0:
                    step(ng, per_n)
            run(cg)
            for il in range(tpg):
                otok_to_mixT(otoks[il], 128, il * 128)
            for tl in tiles:
                stage_wout(tl, xp[tl.r0:tl.r0 + 128, :], X1[tl.r0:tl.r0 + 128, :], r_x1[tl.idx])
            run(ng)

    run(pcg)
    S.barrier()
    GB = 512
    with ExitStack() as stB:
        def BB(name, shape, dtype):
            return B(name, shape, dtype, stB)
        wff1 = BB("wff1", [128, 8, 4096], BF16)
        xinB = Rot([BB(f"xinB{i}", [128, D], F32) for i in range(2)])
        hnB = Rot([BB("hnB0", [128, D], BF16)])
        ssB = Rot([BB(f"ssB{i}", [128, 2], F32) for i in range(2)])
        junkB = BB("junkB", [128, D], BF16)
        hTB = BB("hTB", [128, 8, GB], BF16)
        xoB = Rot([BB(f"xoB{i}", [128, D], F32) for i in range(2)])

        with ExitStack() as stB1:
            def B1(name, shape, dtype):
                return B(name, shape, dtype, stB1)
            wmq = B1("wmq", [128, 8, 512], BF16); wmo = B1("wmo", [128, 4, 1024], BF16)
            wmk = B1("wmk", [128, 8, 512], BF16); wmv = B1("wmv", [128, 8, 512], BF16)
            MKT = B1("MKT", [128, 5, 4, 256], BF16)
            MV = B1("MV", [128, 5, 2, 4, 132], BF16)
            mT = B1("mT", [128, 8, 256], BF16)
            sqm = B1("sqm", [128, 512], F32); ssm = B1("ssm", [128, 8], F32)
            t1m = B1("t1m", [128, 512], F32)
            mkf = Rot([B1(f"mkf{i}", [128, 512], F32) for i in range(2)])
            mkb = B1("mkb", [128, 512], BF16)
            mvf = Rot([B1(f"mvf{i}", [128, 512], F32) for i in range(2)])
            cmkb = B1("cmkb", [128, 2, 512], BF16)
            qmb = B1("qmb", [128, 512], BF16)
            PTm = Rot([B1(f"PTm{i}", [128, 2, GB], BF16) for i in range(2)])
            rlm = Rot([B1(f"rlm{i}", [128, 4], F32) for i in range(2)])

            load_wb(wmk, "wmk", 8)
            load_wb(wmv, "wmv", 8)
            load_wb(wmq, "wmq", 8)
            load_wb(wmo, "wmo", 4)
            load_wb(wff1, "wff1", 8, piece=1024)
            S.pool(lambda e: e.memset(MV.t[:, :, :, :, 128:129], 1.0), writes=[MV.r])

            mtiles = [TT(i, 128, i * 128, i * 128) for i in range(2)]
            stage_norm(mtiles, lambda tl: memp[tl.r0:tl.r0 + 128, :], lambda tl: [], 3, mT, xinB, hnB, ssB, junkB, 3)
            for tl in mtiles:
                for kc in range(8):
                    for which, wt in ((0, wmk), (1, wmv)):
                        S.pe(lambda e, kc=kc, which=which, wt=wt, tl=tl: e.matmul(banks[which][:, :], lhsT=mT.t[:, kc, tl.coff:tl.coff + 128], rhs=wt.t[:, kc, :],
                                                                                  start=(kc == 0), stop=(kc == 7)), reads=[mT.r, wt.r], writes=[rb[which]])
                rstd_groups(banks[0][:, :], rb[0], 128, 4, 128, sqm, ssm)
                kf = mkf.next(); vf = mvf.next()
                S.dve(lambda e: e.tensor_tensor(out=t1m.t[:, :].rearrange("p (g d) -> p g d", d=128), in0=banks[0][:, :].rearrange("p (g d) -> p g d", d=128),
                                                in1=ssm.t[:, 4:8].unsqueeze(2).to_broadcast([128, 4, 128]), op=ALU.mult), reads=[rb[0], ssm.r], writes=[t1m.r])
                S.dve(lambda e, kf=kf: e.tensor_tensor(out=kf.t[:, :].rearrange("p (g d) -> p g d", d=128), in0=t1m.t[:, :].rearrange("p (g d) -> p g d", d=128),
                                                       in1=gmkb.t[:, :].unsqueeze(1).to_broadcast([128, 4, 128]), op=ALU.mult), reads=[t1m.r, gmkb.r], writes=[kf.r])
                S.dma(lambda e, kf=kf, tl=tl: e.dma_start(out=mkp[tl.r0:tl.r0 + 128, :], in_=kf.t[:, :]), reads=[kf.r])
                S.pool(lambda e, kf=kf: e.tensor_copy(out=mkb.t[:, :], in_=kf.t[:, :]), reads=[kf.r], writes=[mkb.r])
                S.act(lambda e, vf=vf: e.activation(out=vf.t[:, :], in_=banks[1][:, :], func=AF.Copy), reads=[rb[1]], writes=[vf.r])
                S.dma(lambda e, vf=vf, tl=tl: e.dma_start(out=mvp[tl.r0:tl.r0 + 128, :], in_=vf.t[:, :]), reads=[vf.r])
                S.pool(lambda e, vf=vf, tl=tl: e.tensor_copy(out=MV.t[:, 0, tl.idx, :, 0:128], in_=vf.t[:, :].rearrange("p (h e) -> p h e", e=128)),
                       reads=[vf.r], writes=[MV.r])
                pb = banks[2][:].bitcast(BF16)
                for h in range(4):
                    S.pe(lambda e, h=h: e.transpose(out=pb[:, h * 128:(h + 1) * 128], in_=mkb.t[:, h * 128:(h + 1) * 128], identity=identb.t[:, :]),
                         reads=[mkb.r, identb.r], writes=[rb[2]])
                S.act(lambda e, tl=tl: e.activation(out=MKT.t[:, 0, :, tl.coff:tl.coff + 128], in_=pb[:, 0:512].rearrange("p (h k) -> p h k", k=128), func=AF.Copy),
                      reads=[rb[2]], writes=[MKT.r])
            for s in range(4):
                S.dma(lambda e, s=s: e.dma_start(out=cmkb.t[:, :, :], in_=cmk[s].rearrange("(t p) c -> p t c", p=128)), writes=[cmkb.r], q="pool")
                for tt in range(2):
                    S.dma(lambda e, s=s, tt=tt: e.dma_start(out=MV.t[:, 1 + s, tt, :, 0:128],
                                                            in_=cmv[s, tt * 128:(tt + 1) * 128, :].rearrange("p (h e) -> p h e", e=128)),
                          writes=[MV.r], q="pool")
                for tt in range(2):
                    pb = banks[2 + tt][:].bitcast(BF16)
                    for h in range(4):
                        S.pe(lambda e, h=h, tt=tt, pb=pb: e.transpose(out=pb[:, h * 128:(h + 1) * 128], in_=cmkb.t[:, tt, h * 128:(h + 1) * 128], identity=identb.t[:, :]),
                             reads=[cmkb.r, identb.r], writes=[rb[2 + tt]])
                    S.act(lambda e, s=s, tt=tt, pb=pb: e.activation(out=MKT.t[:, 1 + s, :, tt * 128:(tt + 1) * 128],
                                                                    in_=pb[:, 0:512].rearrange("p (h k) -> p h k", k=128), func=AF.Copy),
                          reads=[rb[2 + tt]], writes=[MKT.r])

            def cross_s1(tiles, QmT):
                ng = norm_gen(tiles, lambda tl: X1[tl.r0:tl.r0 + tl.rows, :], lambda tl: [r_x1[tl.idx]], 1, hTB, xinB, hnB, ssB, junkB, 0)
                for tl in tiles:
                    rows = tl.rows
                    step(ng, 1)
                    yield
                    step(ng, 1)
                    yield
                    for kc in range(8):
                        S.pe(lambda e, kc=kc, tl=tl, rows=rows: e.matmul(banks[1][0:rows, :], lhsT=hTB.t[:, kc, tl.coff:tl.coff + rows], rhs=wmq.t[:, kc, :],
                                                                         start=(kc == 0), stop=(kc == 7)), reads=[hTB.r, wmq.r], writes=[rb[1]])
                    rstd_groups(banks[1][0:rows, :], rb[1], rows, 4, 128, sqm, ssm)
                    S.dve(lambda e, rows=rows: e.tensor_tensor(out=t1m.t[0:rows, :].rearrange("p (g d) -> p g d", d=128),
                                                               in0=banks[1][0:rows, :].rearrange("p (g d) -> p g d", d=128),
                                                               in1=ssm.t[0:rows, 4:8].unsqueeze(2).to_broadcast([rows, 4, 128]), op=ALU.mult),
                          reads=[rb[1], ssm.r], writes=[t1m.r])
                    S.dve(lambda e, rows=rows: e.tensor_tensor(out=qmb.t[0:rows, :].rearrange("p (g d) -> p g d", d=128),
                                                               in0=t1m.t[0:rows, :].rearrange("p (g d) -> p g d", d=128),
                                                               in1=gmqb.t[0:rows, :].unsqueeze(1).to_broadcast([rows, 4, 128]), op=ALU.mult),
                          reads=[t1m.r, gmqb.r], writes=[qmb.r])
                    yield
                    pb = banks[0][:].bitcast(BF16)
                    for h in range(4):
                        S.pe(lambda e, h=h, rows=rows, pb=pb: e.transpose(out=pb[:, h * 128:h * 128 + rows], in_=qmb.t[0:rows, h * 128:(h + 1) * 128],
                                                                         identity=identb.t[0:rows, 0:rows]), reads=[qmb.r, identb.r], writes=[rb[0]])
                    S.act(lambda e, tl=tl, rows=rows, pb=pb: e.activation(out=QmT.t[:, :, tl.coff:tl.coff + rows],
                                                                          in_=pb[:, 0:512].rearrange("p (h k) -> p h k", k=128)[:, :, 0:rows], func=AF.Copy),
                          reads=[rb[0]], writes=[QmT.r])
                    yield

            def cross_s2(seqs, QmT, omT, otoks):
                mems = {}
                for ui, u in enumerate(seqs):
                    mems.setdefault(u[0], []).append((ui, u))
                for wm, units in mems.items():
                    c_lo = min(u[1] for _, u in units); c_hi = max(u[1] + u[2] for _, u in units)
                    for h in range(4):
                        pt = PTm.next()
                        for mt in range(2):
                            S.pe(lambda e, h=h, mt=mt, wm=wm: e.matmul(banks[2 + mt][:, c_lo:c_hi], lhsT=MKT.t[:, wm, h, mt * 128:(mt + 1) * 128],
                                                                       rhs=QmT.t[:, h, c_lo:c_hi], start=True, stop=True),
                                 reads=[MKT.r, QmT.r], writes=[rb[2 + mt]])
                            S.act(lambda e, mt=mt, pt=pt: e.activation(out=pt.t[:, mt, c_lo:c_hi], in_=banks[2 + mt][:, c_lo:c_hi], func=AF.Exp),
                                  reads=[rb[2 + mt]], writes=[pt.r])
                        yield
                        for k, (ui, u) in enumerate(units):
                            _, c0, ncol = u
                            obk = 4 + (k % 2)
                            for mt in range(2):
                                S.pe(lambda e, mt=mt, pt=pt, c0=c0, ncol=ncol, obk=obk, wm=wm, h=h: e.matmul(
                                    banks[obk][0:ncol, 0:129], lhsT=pt.t[:, mt, c0:c0 + ncol], rhs=MV.t[:, wm, mt, h, 0:129], start=(mt == 0), stop=(mt == 1)),
                                    reads=[pt.r, MV.r], writes=[rb[obk]])
                            rl = rlm.next()
                            ot = otoks[ui]
                            S.dve(lambda e, rl=rl, obk=obk, ncol=ncol: e.reciprocal(out=rl.t[0:ncol, 0:1], in_=banks[obk][0:ncol, 128:129]),
                                  reads=[rb[obk]], writes=[rl.r])
                            S.act(lambda e, rl=rl, obk=obk, ncol=ncol, ot=ot, h=h: e.activation(out=ot.t[0:ncol, h, :], in_=banks[obk][0:ncol, 0:128], func=AF.Copy,
                                                                                                scale=rl.t[0:ncol, 0:1]), reads=[rb[obk], rl.r], writes=[ot.r])
                        yield
                    for ui, u in units:
                        _, c0, ncol = u
                        ot = otoks[ui]
                        pb = banks[0][:].bitcast(BF16)
                        for h in range(4):
                            S.pe(lambda e, h=h, ot=ot, ncol=ncol, pb=pb: e.transpose(out=pb[:, h * 128:h * 128 + ncol], in_=ot.t[0:ncol, h, :],
                                                                                    identity=identb.t[0:ncol, 0:ncol]), reads=[ot.r, identb.r], writes=[rb[0]])
                        S.act(lambda e, c0=c0, ncol=ncol, pb=pb: e.activation(out=omT.t[:, :, c0:c0 + ncol],
                                                                              in_=pb[:, 0:512].rearrange("p (h k) -> p h k", k=128)[:, :, 0:ncol], func=AF.Copy),
                              reads=[rb[0]], writes=[omT.r])
                        yield

            def cross_s3(tiles, omT):
                for tl in tiles:
                    rows = tl.rows
                    xr = xoB.next()
                    S.dma(lambda e, xr=xr, tl=tl, rows=rows: e.dma_start(out=xr.t[0:rows, :], in_=X1[tl.r0:tl.r0 + rows, :]), reads=[r_x1[tl.idx]], writes=[xr.r])
                    for half in range(2):
                        bk = 6 + half
                        for c in range(4):
                            S.pe(lambda e, c=c, half=half, bk=bk, tl=tl, rows=rows: e.matmul(banks[bk][0:rows, :], lhsT=omT.t[:, c, tl.coff:tl.coff + rows],
                                                                                            rhs=wmo.t[:, c, half * 512:(half + 1) * 512], start=(c == 0), stop=(c == 3)),
                                 reads=[omT.r, wmo.r], writes=[rb[bk]])
                        S.dve(lambda e, half=half, bk=bk, xr=xr, rows=rows: e.tensor_tensor(out=xr.t[0:rows, half * 512:(half + 1) * 512],
                                                                                            in0=xr.t[0:rows, half * 512:(half + 1) * 512], in1=banks[bk][0:rows, :], op=ALU.add),
                              reads=[xr.r, rb[bk]], writes=[xr.r])
                    S.dma(lambda e, xr=xr, tl=tl, rows=rows: e.dma_start(out=X2[tl.r0:tl.r0 + rows, :], in_=xr.t[0:rows, :]), reads=[xr.r], writes=[r_x2[tl.idx]])
                    yield

            tpgB = GB // 128
            QmTs = [B1(f"QmT{i}", [128, 4, GB], BF16) for i in range(2)]
            omTs = [B1(f"omT{i}", [128, 4, GB], BF16) for i in range(2)]
            pots = [[B1(f"pot{p}_{i}", [128, 4, 128], BF16) for i in range(tpgB)] for p in range(2)]
            cgroups = []
            stl = TT(NT, 64, SEQ, 0)
            cgroups.append(([stl], [(1 + sq_, sq_ * 16, 16) for sq_ in range(4)]))
            for g in range(SEQ // GB):
                tiles = [TT(g * tpgB + i, 128, (g * tpgB + i) * 128, i * 128) for i in range(tpgB)]
                cgroups.append((tiles, [(0, i * 128, 128) for i in range(tpgB)]))
            ngr = len(cgroups)
            for t in range(ngr + 2):
                gens = []
                if t < ngr:
                    gens.append(cross_s1(cgroups[t][0], QmTs[t % 2]))
                if 0 <= t - 1 < ngr:
                    gens.append(cross_s2(cgroups[t - 1][1], QmTs[(t - 1) % 2], omTs[(t - 1) % 2], pots[(t - 1) % 2]))
                if 0 <= t - 2 < ngr:
                    gens.append(cross_s3(cgroups[t - 2][0], omTs[(t - 2) % 2]))
                alive = list(gens)
                while alive:
                    alive = [gq_ for gq_ in alive if step(gq_)]

        S.barrier()
        with ExitStack() as stB2:
            def B2(name, shape, dtype):
                return B(name, shape, dtype, stB2)
            wff2 = B2("wff2", [128, 32, 1024], BF16)
            hidT = B2("hidT", [128, 32, GB], BF16)
            reluT = Rot([B2(f"reluT{i}", [128, GB], F32) for i in range(2)])
            load_wb(wff2, "wff2", 32, piece=512)

            def ffn1(n):
                for f in range(32):
                    bk = f % 2
                    for kc in range(8):
                        S.pe(lambda e, f=f, kc=kc, bk=bk: e.matmul(banks[bk][:, 0:n], lhsT=wff1.t[:, kc, f * 128:(f + 1) * 128], rhs=hTB.t[:, kc, 0:n],
                                                                   start=(kc == 0), stop=(kc == 7)), reads=[wff1.r, hTB.r], writes=[rb[bk]])
                    rt = reluT.next()
                    S.act(lambda e, bk=bk, rt=rt: e.activation(out=rt.t[:, 0:n], in_=banks[bk][:, 0:n], func=AF.Relu), reads=[rb[bk]], writes=[rt.r])
                    S.dve(lambda e, f=f, rt=rt: e.tensor_tensor(out=hidT.t[:, f, 0:n], in0=rt.t[:, 0:n], in1=rt.t[:, 0:n], op=ALU.mult),
                          reads=[rt.r], writes=[hidT.r])

            def ffn2_gen(tiles, dst_fn):
                for ti, tl in enumerate(tiles):
                    rows = tl.rows
                    xr = xoB.next()
                    S.dma(lambda e, xr=xr, tl=tl, rows=rows: e.dma_start(out=xr.t[0:rows, :], in_=X2[tl.r0:tl.r0 + rows, :]), reads=[r_x2[tl.idx]], writes=[xr.r])
                    for half in range(2):
                        bk = 2 + (ti % 2) * 2 + half
                        for f in range(32):
                            S.pe(lambda e, f=f, half=half, bk=bk, tl=tl, rows=rows: e.matmul(banks[bk][0:rows, :], lhsT=hidT.t[:, f, tl.coff:tl.coff + rows],
                                                                                            rhs=wff2.t[:, f, half * 512:(half + 1) * 512], start=(f == 0), stop=(f == 31)),
                                 reads=[hidT.r, wff2.r], writes=[rb[bk]])
                        S.dve(lambda e, half=half, bk=bk, xr=xr, rows=rows: e.tensor_tensor(out=xr.t[0:rows, half * 512:(half + 1) * 512],
                                                                                            in0=xr.t[0:rows, half * 512:(half + 1) * 512], in1=banks[bk][0:rows, :], op=ALU.add),
                              reads=[xr.r, rb[bk]], writes=[xr.r])
                    S.dma(lambda e, xr=xr, tl=tl, rows=rows: e.dma_start(out=dst_fn(tl), in_=xr.t[0:rows, :]), reads=[xr.r])
                    yield

            def x2norm(tiles):
                return norm_gen(tiles, lambda tl: X2[tl.r0:tl.r0 + tl.rows, :], lambda tl: [r_x2[tl.idx]], 2, hTB, xinB, hnB, ssB, junkB, 6)

            fgroups = [([TT(NT, 64, SEQ, 0)], 64, lambda tl: ys[0:64, :])]
            for g in range(SEQ // GB):
                fgroups.append(([TT(g * tpgB + i, 128, (g * tpgB + i) * 128, i * 128) for i in range(tpgB)], GB, lambda tl: yp[tl.r0:tl.r0 + 128, :]))
            run(x2norm(fgroups[0][0]))
            for gi, (tiles, n, dst_fn) in enumerate(fgroups):
                ffn1(n)
                ng = x2norm(fgroups[gi + 1][0]) if gi + 1 < len(fgroups) else iter(())
                step(ng, 1)
                f2 = ffn2_gen(tiles, dst_fn)
                while step(f2):
                    step(ng, 2)
                run(ng)

    S.finalize()
    print("ops per engine:", {k: len(v) for k, v in S.ops.items()}, flush=True)
    return nc


_NC_CACHE = {}


def kernel(x_prompt, x_sample, cache_k, cache_v, cache_conv, cache_mem_k, cache_mem_v, mem_prompt, rel_table,
           g_mix, w_in, g_q, g_k, lam_vec, g_sub, w_conv, b_conv, ln_g, ln_b, w_out, g_cross, g_mem,
           w_mq, w_mk, w_mv, g_mq, g_mk, w_mo, g_ffn, w_ff1, w_ff2):
    f = lambda a: np.ascontiguousarray(np.asarray(a, dtype=np.float32))
    x_prompt, x_sample = f(x_prompt), f(x_sample)
    cache_k, cache_v, cache_conv = f(cache_k), f(cache_v), f(cache_conv)
    cache_mem_k, cache_mem_v, mem_prompt = f(cache_mem_k), f(cache_mem_v), f(mem_prompt)
    rel = np.arange(-255, 128, dtype=np.int32)
    bkt = rel_bucket_np(rel)
    ohE = np.zeros((32, 383), np.float32)
    ohE[bkt, np.arange(383)] = 1.0
    shared = {
        "rel_table": f(rel_table), "ohE": ohE,
        "g_mix": f(g_mix).reshape(1, D), "w_in": f(w_in)[0], "g_q": f(g_q).reshape(1, 64), "g_k": f(g_k).reshape(1, 64),
        "lam_vec": f(lam_vec).reshape(1, 256), "g_sub": f(g_sub).reshape(1, 128), "w_conv": f(w_conv)[0],
        "b_conv": f(b_conv).reshape(1, 512), "ln_g": f(ln_g).reshape(1, 512), "ln_b": f(ln_b).reshape(1, 512),
        "w_out": f(w_out)[0], "g_cross": f(g_cross).reshape(1, D), "g_mem": f(g_mem).reshape(1, D),
        "w_mq": f(w_mq)[0], "w_mk": f(w_mk)[0], "w_mv": f(w_mv)[0], "g_mq": f(g_mq).reshape(1, 128), "g_mk": f(g_mk).reshape(1, 128),
        "w_mo": f(w_mo)[0], "g_ffn": f(g_ffn).reshape(1, D), "w_ff1": f(w_ff1)[0], "w_ff2": f(w_ff2)[0],
    }
    in_maps = []
    for c in range(NCORES):
        m = dict(shared)
        sl = slice(4 * c, 4 * c + 4)
        m["xp"] = x_prompt[c]
        m["xs"] = x_sample[sl].reshape(64, D)
        m["ck"] = cache_k[0, sl].reshape(4, PAST, 512)
        m["cv"] = cache_v[0, sl].reshape(4, PAST, 512)
        m["cc"] = cache_conv[0, sl]
        m["cmk"] = cache_mem_k[0, sl].reshape(4, 256, 512)
        m["cmv"] = cache_mem_v[0, sl].reshape(4, 256, 512)
        m["memp"] = mem_prompt[c]
        in_maps.append(m)
    if "nc" not in _NC_CACHE:
        _NC_CACHE["nc"] = build_program()
    nc = _NC_CACHE["nc"]
    res = run_bass_kernel_spmd(nc, in_maps, core_ids=list(range(NCORES)))
    R = res.results
    st = lambda k: np.stack([np.asarray(R[c][k], dtype=np.float32) for c in range(NCORES)])
    cat = lambda k: np.concatenate([np.asarray(R[c][k], dtype=np.float32) for c in range(NCORES)], axis=0)
    y_prompt = st("yp")
    y_sample = cat("ys").reshape(32, 16, D)
    k_prompt = st("kp").reshape(1, 8, SEQ, 4, 2, 64)
    v_prompt = st("vp").reshape(1, 8, SEQ, 4, 128)
    conv_prompt = st("cpo").reshape(1, 8, 30, 512)
    mem_k_prompt = st("mkp").reshape(1, 8, 256, 4, 128)
    mem_v_prompt = st("mvp").reshape(1, 8, 256, 4, 128)
    k_sample = cat("kso").reshape(1, 32, 16, 4, 2, 64)
    v_sample = cat("vso").reshape(1, 32, 16, 4, 128)
    conv_sample = cat("cso").reshape(1, 32, 30, 512)
    return (y_prompt, y_sample, k_prompt, v_prompt, conv_prompt, mem_k_prompt, mem_v_prompt, k_sample, v_sample, conv_sample)
```
